# Optimizing a Trainium2 kernel written in Bass

```python
import jax, jax.numpy as jnp
from jax import lax
import numpy as np

D_MODEL = 1024
BATCH = 16
SEQ = 2048
DEPTH = 2
DEC_BATCH = 32
DEC_SEQ = 8
PAST_LEN = 16384
PAGE_SIZE = 128

HEAD_DIM = 64
D_SB = D_MODEL // 2
SB_HEADS = D_SB // HEAD_DIM
D_RW = D_MODEL - D_SB
RW_HEADS = D_RW // HEAD_DIM
LORA_W = 64
LORA_A = 64
LORA_G = 128
P_RW = 3 * D_RW + LORA_W + LORA_A + LORA_G
P_AB = 3 * D_SB + P_RW
D_CONV = D_MODEL
CONV_W = 3
D_FF = -(-8 * D_MODEL // (3 * 256)) * 256
Q_BLOCK = 128
N_AB_LAYERS = (DEPTH + 1) // 2
N_CONV_LAYERS = DEPTH // 2
RMS_EPS = 1e-6
GN_EPS = 64e-5
SB_BIAS_INIT = -8.0

kernel_name = 'stickbreak_rwkv7_shortconv_hybrid_step'


def rmsnorm(x, g):
    xf = x.astype(jnp.float32)
    return (xf * lax.rsqrt(jnp.mean(xf * xf, -1, keepdims=True) + RMS_EPS) * g).astype(x.dtype)


def heads(t, n):
    return t.reshape(t.shape[:-1] + (n, HEAD_DIM))


def split_ab(p, qg, kg):
    q = rmsnorm(heads(p[..., :D_SB], SB_HEADS), qg)
    k = rmsnorm(heads(p[..., D_SB:2 * D_SB], SB_HEADS), kg)
    v = heads(p[..., 2 * D_SB:3 * D_SB], SB_HEADS)
    return q, k, v, p[..., 3 * D_SB:]


def sb_weights(z, mask):
    z = z.astype(jnp.float32)
    log_keep = jnp.where(mask, jax.nn.log_sigmoid(-z), 0.0)
    after = lax.cumsum(log_keep, axis=z.ndim - 1, reverse=True) - log_keep
    return jnp.where(mask, jnp.exp(jax.nn.log_sigmoid(z) + after), 0.0)


def sb_attend_prompt(q, k, v, bias):
    b, s = q.shape[0], q.shape[1]
    nblk = s // Q_BLOCK
    qb = q.reshape(b, nblk, Q_BLOCK, SB_HEADS, HEAD_DIM).transpose(1, 0, 2, 3, 4)
    posb = jnp.arange(s).reshape(nblk, Q_BLOCK)
    kpos = jnp.arange(s)
    scale = HEAD_DIM ** -0.5
    bias_h = bias.astype(jnp.float32)[None, :, None, None]

    def block(args):
        qi, pi = args
        z = jnp.einsum('bqhd,bkhd->bhqk', qi, k).astype(jnp.float32) * scale + bias_h
        w = sb_weights(z, kpos[None, :] < pi[:, None])
        return jnp.einsum('bhqk,bkhd->bqhd', w.astype(v.dtype), v)

    o = lax.map(block, (qb, posb))
    return o.transpose(1, 0, 2, 3, 4).reshape(b, s, D_SB)


def sb_attend_sample(q, k_new, v_new, k_past, v_past, bias):
    b, t = q.shape[0], q.shape[1]
    p = k_past.shape[1]
    scale = HEAD_DIM ** -0.5
    z = jnp.concatenate([jnp.einsum('bqhd,bkhd->bhqk', q, k_past),
                         jnp.einsum('bqhd,bkhd->bhqk', q, k_new)], -1).astype(jnp.float32) * scale
    z = z + bias.astype(jnp.float32)[None, :, None, None]
    ti = jnp.arange(t)
    mask = jnp.concatenate([jnp.ones((t, p), bool), ti[None, :] < ti[:, None]], -1)
    w = sb_weights(z, mask).astype(v_new.dtype)
    o = jnp.einsum('bhqk,bkhd->bqhd', w[..., :p], v_past) + jnp.einsum('bhqk,bkhd->bqhd', w[..., p:], v_new)
    return o.reshape(b, t, D_SB)


def rwkv_mix(p, shift_prev, s0, mu, w0, w2, a0, a2, g2, k_k, k_a, r_k, lnx_w, lnx_b):
    b, t = p.shape[0], p.shape[1]
    f32 = jnp.float32
    prev = jnp.concatenate([shift_prev[:, None].astype(p.dtype), p[:, :-1]], 1)
    xm = (p + (prev - p) * mu).astype(f32)
    o1, o2, o3 = D_RW, 2 * D_RW, 3 * D_RW
    r, k, v = xm[..., :o1], xm[..., o1:o2], xm[..., o2:o3]
    pw = xm[..., o3:o3 + LORA_W]
    pa = xm[..., o3 + LORA_W:o3 + LORA_W + LORA_A]
    pg = xm[..., o3 + LORA_W + LORA_A:]
    log_w = -jax.nn.softplus(-(w0 + jnp.tanh(pw) @ w2)) - 0.5
    decay = jnp.exp(-jnp.exp(log_w))
    a = jax.nn.sigmoid(a0 + pa @ a2)
    g = jax.nn.sigmoid(pg) @ g2
    kk = heads(k * k_k, RW_HEADS)
    kk = kk * lax.rsqrt(jnp.maximum(jnp.sum(kk * kk, -1, keepdims=True), 1e-24))
    k = k * (1.0 + (a - 1.0) * k_a)
    rh, kh, vh = heads(r, RW_HEADS), heads(k, RW_HEADS), heads(v, RW_HEADS)
    dh, ah = heads(decay, RW_HEADS), heads(a, RW_HEADS)

    def step(s, inp):
        r_t, w_t, k_t, v_t, kk_t, a_t = inp
        s_kk = jnp.einsum('bhij,bhj->bhi', s, kk_t)
        s = (s * w_t[..., None, :] - s_kk[..., :, None] * (kk_t * a_t)[..., None, :]
             + v_t[..., :, None] * k_t[..., None, :])
        return s, jnp.einsum('bhij,bhj->bhi', s, r_t)

    seqs = tuple(jnp.moveaxis(z, 1, 0) for z in (rh, dh, kh, vh, kk, ah))
    s_t, y = lax.scan(step, s0.astype(f32), seqs)
    y = jnp.moveaxis(y, 0, 1)
    mean = jnp.mean(y, -1, keepdims=True)
    var = jnp.mean(jnp.square(y - mean), -1, keepdims=True)
    y = ((y - mean) * lax.rsqrt(var + GN_EPS)).reshape(b, t, D_RW) * lnx_w + lnx_b
    bonus = (jnp.sum(rh * kh * r_k, -1, keepdims=True) * vh).reshape(b, t, D_RW)
    out = (y + bonus) * g
    return out.astype(p.dtype), p[:, -1], s_t


def conv_mix(h, buf, w_in, conv_w, w_out):
    t = h.shape[1]
    pr = h @ w_in
    bg, cg, hv = pr[..., :D_CONV], pr[..., D_CONV:2 * D_CONV], pr[..., 2 * D_CONV:]
    u = cg * hv
    full = jnp.concatenate([buf.astype(u.dtype), u], 1)
    y = full[:, 0:t] * conv_w[0]
    for i in range(1, CONV_W):
        y = y + full[:, i:i + t] * conv_w[i]
    return (bg * y) @ w_out, full[:, t:]


def swiglu(h, wg, wu, wd):
    return (jax.nn.silu(h @ wg) * (h @ wu)) @ wd


def setup_inputs(seed: int = 0) -> dict:
    key = jax.random.key(seed)
    ks = jax.random.split(key, 40)
    f32 = jnp.float32
    nrm = lambda i, shape, sc: jax.random.normal(ks[i], shape, f32) * sc
    n_pages = PAST_LEN // PAGE_SIZE
    n_used = DEC_BATCH * n_pages
    n_phys = n_used + n_used // 4
    page_table = jax.random.permutation(ks[3], n_phys)[:n_used].reshape(DEC_BATCH, n_pages).astype(jnp.int32)
    return {
        'x_prompt': nrm(0, (BATCH, SEQ, D_MODEL), 1.0),
        'x_sample': nrm(1, (DEC_BATCH, DEC_SEQ, D_MODEL), 1.0),
        'cache_k': nrm(2, (N_AB_LAYERS, n_phys, PAGE_SIZE, SB_HEADS, HEAD_DIM), 1.0),
        'cache_v': nrm(4, (N_AB_LAYERS, n_phys, PAGE_SIZE, SB_HEADS, HEAD_DIM), 1.0),
        'state_wkv': nrm(5, (N_AB_LAYERS, DEC_BATCH, RW_HEADS, HEAD_DIM, HEAD_DIM), 0.3),
        'state_shift': nrm(6, (N_AB_LAYERS, DEC_BATCH, P_RW), 1.0),
        'state_conv': nrm(7, (N_CONV_LAYERS, DEC_BATCH, CONV_W - 1, D_CONV), 1.0),
        'page_table': page_table,
        'norm_mix': 1.0 + nrm(8, (DEPTH, D_MODEL), 0.02),
        'norm_ffn': 1.0 + nrm(9, (DEPTH, D_MODEL), 0.02),
        'w_in_ab': nrm(10, (N_AB_LAYERS, D_MODEL, P_AB), D_MODEL ** -0.5),
        'q_norm': 1.0 + nrm(11, (N_AB_LAYERS, HEAD_DIM), 0.02),
        'k_norm': 1.0 + nrm(12, (N_AB_LAYERS, HEAD_DIM), 0.02),
        'sb_bias': SB_BIAS_INIT + nrm(31, (N_AB_LAYERS, SB_HEADS), 0.1),
        'mu_rw': jax.random.uniform(ks[13], (N_AB_LAYERS, P_RW), f32),
        'w0': nrm(14, (N_AB_LAYERS, D_RW), 0.5),
        'w2': nrm(15, (N_AB_LAYERS, LORA_W, D_RW), LORA_W ** -0.5),
        'a0': nrm(16, (N_AB_LAYERS, D_RW), 0.5),
        'a2': nrm(17, (N_AB_LAYERS, LORA_A, D_RW), LORA_A ** -0.5),
        'g2': nrm(18, (N_AB_LAYERS, LORA_G, D_RW), LORA_G ** -0.5),
        'k_k': 0.85 + nrm(19, (N_AB_LAYERS, D_RW), 0.05),
        'k_a': 1.0 + nrm(20, (N_AB_LAYERS, D_RW), 0.05),
        'r_k': nrm(21, (N_AB_LAYERS, RW_HEADS, HEAD_DIM), 0.1),
        'lnx_w': 1.0 + nrm(22, (N_AB_LAYERS, D_RW), 0.02),
        'lnx_b': nrm(23, (N_AB_LAYERS, D_RW), 0.02),
        'w_out_ab': nrm(24, (N_AB_LAYERS, D_SB + D_RW, D_MODEL), (D_SB + D_RW) ** -0.5),
        'w_in_c': nrm(25, (N_CONV_LAYERS, D_MODEL, 3 * D_CONV), D_MODEL ** -0.5),
        'conv_w': nrm(26, (N_CONV_LAYERS, CONV_W, D_CONV), CONV_W ** -0.5),
        'w_out_c': nrm(27, (N_CONV_LAYERS, D_CONV, D_MODEL), D_CONV ** -0.5),
        'w_gate': nrm(28, (DEPTH, D_MODEL, D_FF), D_MODEL ** -0.5),
        'w_up': nrm(29, (DEPTH, D_MODEL, D_FF), D_MODEL ** -0.5),
        'w_down': nrm(30, (DEPTH, D_FF, D_MODEL), D_FF ** -0.5),
    }


def reference(x_prompt, x_sample, cache_k, cache_v, state_wkv, state_shift, state_conv, page_table,
              norm_mix, norm_ffn, w_in_ab, q_norm, k_norm, sb_bias, mu_rw, w0, w2, a0, a2, g2, k_k, k_a, r_k,
              lnx_w, lnx_b, w_out_ab, w_in_c, conv_w, w_out_c, w_gate, w_up, w_down):
    bp = x_prompt.shape[0]
    bs = x_sample.shape[0]
    past = page_table.shape[1] * PAGE_SIZE
    xp, xs = x_prompt, x_sample
    kp_l, vp_l, ks_l, vs_l = [], [], [], []
    wp_l, ws_l, sp_l, ss_l = [], [], [], []
    cp_l, cs_l = [], []
    for layer in range(DEPTH):
        j = layer // 2
        hp = rmsnorm(xp, norm_mix[layer])
        hs = rmsnorm(xs, norm_mix[layer])
        if layer % 2 == 0:
            rw = (mu_rw[j], w0[j], w2[j], a0[j], a2[j], g2[j], k_k[j], k_a[j], r_k[j], lnx_w[j], lnx_b[j])
            q, k, v, prw = split_ab(hp @ w_in_ab[j], q_norm[j], k_norm[j])
            o_sb = sb_attend_prompt(q, k, v, sb_bias[j])
            o_rw, sh, wkv = rwkv_mix(prw, jnp.zeros((bp, P_RW), prw.dtype),
                                     jnp.zeros((bp, RW_HEADS, HEAD_DIM, HEAD_DIM), jnp.float32), *rw)
            xp = xp + jnp.concatenate([o_sb, o_rw], -1) @ w_out_ab[j]
            kp_l.append(k)
            vp_l.append(v)
            wp_l.append(wkv)
            sp_l.append(sh)
            q, k, v, prw = split_ab(hs @ w_in_ab[j], q_norm[j], k_norm[j])
            k_past = cache_k[j][page_table].reshape(bs, past, SB_HEADS, HEAD_DIM)
            v_past = cache_v[j][page_table].reshape(bs, past, SB_HEADS, HEAD_DIM)
            o_sb = sb_attend_sample(q, k, v, k_past, v_past, sb_bias[j])
            o_rw, sh, wkv = rwkv_mix(prw, state_shift[j], state_wkv[j], *rw)
            xs = xs + jnp.concatenate([o_sb, o_rw], -1) @ w_out_ab[j]
            ks_l.append(k)
            vs_l.append(v)
            ws_l.append(wkv)
            ss_l.append(sh)
        else:
            o, buf = conv_mix(hp, jnp.zeros((bp, CONV_W - 1, D_CONV), hp.dtype), w_in_c[j], conv_w[j], w_out_c[j])
            xp = xp + o
            cp_l.append(buf)
            o, buf = conv_mix(hs, state_conv[j], w_in_c[j], conv_w[j], w_out_c[j])
            xs = xs + o
            cs_l.append(buf)
        xp = xp + swiglu(rmsnorm(xp, norm_ffn[layer]), w_gate[layer], w_up[layer], w_down[layer])
        xs = xs + swiglu(rmsnorm(xs, norm_ffn[layer]), w_gate[layer], w_up[layer], w_down[layer])
    return (xp, xs, jnp.stack(kp_l), jnp.stack(vp_l), jnp.stack(ks_l), jnp.stack(vs_l),
            jnp.stack(wp_l), jnp.stack(ws_l), jnp.stack(sp_l), jnp.stack(ss_l),
            jnp.stack(cp_l), jnp.stack(cs_l))
```

```python
import numpy as np
import concourse.bass as bass
import concourse.mybir as mybir
from contextlib import ExitStack

F32 = mybir.dt.float32
BF16 = mybir.dt.bfloat16
I32 = mybir.dt.int32
AF = mybir.ActivationFunctionType
ALU = mybir.AluOpType
AX = mybir.AxisListType

NDMASEM = 16


class T:
    def __init__(self, ap, name=""):
        self.ap = ap
        self.name = name
        self.w = []
        self.r = []

    def __getitem__(self, idx):
        return self.ap[idx]


class Eng:
    def __init__(self, name, handle, sem):
        self.name = name
        self.h = handle
        self.sem = sem
        self.count = 0
        self.seen = {}
        self.ops = []


class Prog:
    def __init__(self, nc, es):
        self.nc = nc
        self.es = es
        self.eng = {}
        for name, h in [("pe", nc.tensor), ("act", nc.scalar), ("dve", nc.vector),
                        ("pool", nc.gpsimd), ("sp", nc.sync)]:
            sem = es.enter_context(nc.semaphore("sem_" + name))
            self.eng[name] = Eng(name, h, sem)
        self.dsem = [es.enter_context(nc.semaphore("dsem%d" % i)) for i in range(2 * NDMASEM)]
        self.dval = [0] * (2 * NDMASEM)
        self.di = {"sp": 0, "pool": 0}
        self.nops = 0

    def _need(self, e, dep):
        kind, key, val = dep
        k = (kind, key)
        if e.seen.get(k, 0) >= val:
            return
        e.seen[k] = val
        if kind == "e":
            sem = self.eng[key].sem
        else:
            sem = self.dsem[key]
        e.ops.append(lambda h, sem=sem, val=val: h.wait_ge(sem, val))

    def _deps(self, e, reads, writes, skip_self=False):
        for t in reads:
            for d in t.w:
                if skip_self and d[0] == "e" and d[1] == e.name:
                    continue
                self._need(e, d)
        for t in writes:
            for d in t.w + t.r:
                if skip_self and d[0] == "e" and d[1] == e.name:
                    continue
                self._need(e, d)

    def op(self, engname, fn, r=(), w=(), skip_self=False):
        e = self.eng[engname]
        self._deps(e, r, w, skip_self=skip_self)
        e.count += 1
        cnt = e.count
        sem = e.sem
        e.ops.append(lambda h, fn=fn, sem=sem: fn(h).then_inc(sem, 1))
        dep = ("e", engname, cnt)
        e.seen[("e", engname)] = max(e.seen.get(("e", engname), 0), 0)
        for t in w:
            t.w = [dep]
            t.r = []
        for t in r:
            if t not in w:
                t.r = _compress(t.r + [dep])
        self.nops += 1

    def dma(self, engname, out, in_, r=(), w=(), **kw):
        e = self.eng[engname]
        self._deps(e, r, w)
        base = 0 if engname == "sp" else NDMASEM
        idx = base + self.di[engname] % NDMASEM
        self.di[engname] += 1
        if self.dval[idx] > 0:
            self._need(e, ("d", idx, self.dval[idx]))
        self.dval[idx] += 16
        val = self.dval[idx]
        sem = self.dsem[idx]
        e.ops.append(lambda h, out=out, in_=in_, sem=sem, kw=kw: h.dma_start(out=out, in_=in_, **kw).then_inc(sem, 16))
        dep = ("d", idx, val)
        for t in w:
            t.w = [dep]
            t.r = []
        for t in r:
            if t not in w:
                t.r = _compress(t.r + [dep])
        self.nops += 1
        return dep

    def coll(self, engname, fn, r=(), w=()):
        e = self.eng[engname]
        self._deps(e, r, w)
        e.count += 1
        cnt = e.count
        sem = e.sem
        e.ops.append(lambda h, fn=fn, sem=sem: fn(h).then_inc(sem))
        dep = ("e", engname, cnt)
        for t in w:
            t.w = [dep]
            t.r = []
        for t in r:
            if t not in w:
                t.r = _compress(t.r + [dep])

    def custom_dma(self, engname, fn, r=(), w=()):
        e = self.eng[engname]
        self._deps(e, r, w)
        base = 0 if engname == "sp" else NDMASEM
        idx = base + self.di[engname] % NDMASEM
        self.di[engname] += 1
        if self.dval[idx] > 0:
            self._need(e, ("d", idx, self.dval[idx]))
        self.dval[idx] += 16
        val = self.dval[idx]
        sem = self.dsem[idx]
        e.ops.append(lambda h, fn=fn, sem=sem: fn(h).then_inc(sem, 16))
        dep = ("d", idx, val)
        for t in w:
            t.w = [dep]
            t.r = []
        for t in r:
            if t not in w:
                t.r = _compress(t.r + [dep])
        return dep

    def finish(self, tiles):
        e = self.eng["sp"]
        for t in tiles:
            for d in t.w + t.r:
                self._need(e, d)
        for idx in range(2 * NDMASEM):
            if self.dval[idx] > 0:
                self._need(e, ("d", idx, self.dval[idx]))

    def emit(self):
        nc = self.nc
        with nc.Block() as block:
            @block.tensor
            def _(h):
                for f in self.eng["pe"].ops:
                    f(h)

            @block.scalar
            def _(h):
                for f in self.eng["act"].ops:
                    f(h)

            @block.vector
            def _(h):
                for f in self.eng["dve"].ops:
                    f(h)

            @block.gpsimd
            def _(h):
                for f in self.eng["pool"].ops:
                    f(h)

            @block.sync
            def _(h):
                for f in self.eng["sp"].ops:
                    f(h)


class Arena:
    def __init__(self, nc, es, name, nwords):
        self.t = es.enter_context(nc.sbuf_tensor(name, [128, nwords], F32))
        self.nwords = nwords
        self.live = []
        self.top = 0
        self.marks = []

    def mark(self):
        self.marks.append(self.top)

    def release(self):
        self.top = self.marks.pop()

    def alloc(self, shape, dtype=F32, name="", parts=128):
        n = int(np.prod(shape))
        if dtype == BF16:
            words = (n + 1) // 2
        else:
            words = n
        lo = (self.top + 15) // 16 * 16
        hi = lo + words
        hi_al = (hi + 15) // 16 * 16
        assert hi_al <= self.nwords, ("arena overflow", name, hi_al, self.nwords)
        self.top = hi_al
        ap = self.t[0:parts, lo:hi]
        if dtype != F32:
            ap = ap.bitcast(dtype)
            if dtype == BF16 and n % 2:
                ap = ap[:, 0:n]
        if len(shape) == 2:
            ap = ap.rearrange("p (a b) -> p a b", a=shape[0])
        elif len(shape) == 3:
            ap = ap.rearrange("p (a b c) -> p a b c", a=shape[0], b=shape[1])
        t = T(ap, name)
        keep = []
        for (l, h, ot) in self.live:
            if l < hi_al and lo < h:
                t.w = t.w + ot.w
                t.r = t.r + ot.r
                if l < lo or h > hi_al:
                    keep.append((l, h, ot))
            else:
                keep.append((l, h, ot))
        t.r = t.r + t.w
        t.w = []
        t.r = _compress(t.r)
        keep.append((lo, hi_al, t))
        self.live = keep
        return t


def _compress(deps):
    best = {}
    for d in deps:
        k = (d[0], d[1])
        if k not in best or best[k] < d[2]:
            best[k] = d[2]
    return [(k[0], k[1], v) for k, v in best.items()]

D = 1024; S = 2048; NH = 8; HD = 64; PAB = 3328; PRW = 1792; DFF = 2816
EPS = 1e-6


def build(cfg):
    nc = bass.Bass("TRN2", target_bir_lowering=False)
    es = ExitStack()
    with es:
        _build(nc, es, cfg)
    return nc


def dram(nc, name, shape, dt, kind):
    return nc.dram_tensor(name, list(shape), dt, kind=kind).ap()


def _build(nc, es, cfg):
    P = Prog(nc, es)
    NSEQ = cfg.get("nseq", 2)
    IN = lambda name, shape, dt=F32: dram(nc, name, shape, dt, "ExternalInput")
    OUT = lambda name, shape, dt=F32: dram(nc, name, shape, dt, "ExternalOutput")
    xp = IN("xp", [2, S, D])
    w_in_ab = IN("w_in_ab", [D, PAB])
    norm_mix = IN("norm_mix", [2, D])
    norm_ffn = IN("norm_ffn", [2, D])
    q_norm = IN("q_norm", [HD])
    k_norm = IN("k_norm", [HD])
    sb_bias = IN("sb_bias", [NH])
    Wd = {}
    Wd["mu_rw"] = IN("mu_rw", [PRW])
    for nm in ["w0", "a0", "k_k", "k_a", "r_k", "lnx_w", "lnx_b"]:
        Wd[nm] = IN(nm, [512])
    Wd["w2"] = IN("w2", [64, 512])
    Wd["a2"] = IN("a2", [64, 512])
    Wd["g2"] = IN("g2", [128, 512])
    w_out_ab = IN("w_out_ab", [D, D])
    w_in_c = IN("w_in_c", [D, 3 * D])
    conv_w = IN("conv_w", [3, D])
    w_out_c = IN("w_out_c", [D, D])
    w_gate = IN("w_gate", [2, D, DFF])
    w_up = IN("w_up", [2, D, DFF])
    w_down = IN("w_down", [2, DFF, D])
    y_p = OUT("y_p", [2, S, D])
    k_p = OUT("k_p", [2, S, 512])
    v_p = OUT("v_p", [2, S, 512])
    wkv_p = OUT("wkv_p", [2, 8, 64, 64])
    shift_p = OUT("shift_p", [2, PRW])
    conv_p = OUT("conv_p", [2, 2, D])
    prw_d = dram(nc, "prw_d", [2, 14, 128, S], F32, "Internal")
    d_prw = [T(None, "prw%d" % s) for s in range(2)]
    d_out = T(None, "outs")

    xT_t = es.enter_context(nc.sbuf_tensor("xT", [128, 8, S], F32))
    xT = [T(xT_t[:, :, b * 512:(b + 1) * 512], "xT%d" % b) for b in range(4)]
    cst = Arena(nc, es, "cst", 4096)
    A = Arena(nc, es, "arena", 31808)
    ps_t = [es.enter_context(nc.psum_tensor("ps%d" % i, [128, 512], F32)) for i in range(8)]
    ps = [T(ps_t[i][:, :], "ps%d" % i) for i in range(8)]
    psb = [ps_t[i][:, :].bitcast(BF16) for i in range(8)]

    io_i = cst.alloc([128], I32, "iota_i")
    P.op("pool", lambda h: h.iota(io_i[:, :], pattern=[[1, 128]], base=0, channel_multiplier=-1), w=[io_i])
    io_f = cst.alloc([128], F32, "iota_f")
    P.op("dve", lambda h: h.tensor_copy(io_f[:, :], io_i[:, :]), r=[io_i], w=[io_f])
    ident_f = cst.alloc([128], F32, "ident_f")
    P.op("dve", lambda h: h.tensor_single_scalar(ident_f[:, :], io_f[:, :], 0.0, ALU.is_equal), r=[io_f], w=[ident_f])
    ident_b = cst.alloc([128], BF16, "ident_b")
    P.op("dve", lambda h: h.tensor_single_scalar(ident_b[:, :], io_f[:, :], 0.0, ALU.is_equal), r=[io_f], w=[ident_b])
    maskT = cst.alloc([128], BF16, "maskT")
    P.op("dve", lambda h: h.tensor_single_scalar(maskT[:, :], io_f[:, :], 0.0, ALU.is_gt), r=[io_f], w=[maskT])
    negU = cst.alloc([128], BF16, "negU")
    P.op("dve", lambda h: h.tensor_scalar(negU[:, :], io_f[:, :], 0.0, -1.0, ALU.is_le, ALU.mult), r=[io_f], w=[negU])
    neg1 = cst.alloc([128], BF16, "neg1")
    P.op("dve", lambda h: h.memset(neg1[:, :], -1.0), w=[neg1])
    ones_b = cst.alloc([128], BF16, "ones_b")
    P.op("dve", lambda h: h.memset(ones_b[:, :], 1.0), w=[ones_b])
    bd_b = cst.alloc([128], BF16, "bd_b")
    P.op("dve", lambda h: h.memset(bd_b[:, :], 0.0), w=[bd_b])
    P.op("dve", lambda h: h.memset(bd_b[0:64, 0:64], 1.0), w=[bd_b])
    P.op("dve", lambda h: h.memset(bd_b[64:128, 64:128], 1.0), w=[bd_b])
    epsc = cst.alloc([1], F32, "eps")
    P.op("dve", lambda h: h.memset(epsc[:, :], EPS), w=[epsc])
    g_mix = cst.alloc([2, 8], F32, "g_mix")
    g_ffn = cst.alloc([2, 8], F32, "g_ffn")
    for l in range(2):
        P.dma("sp", g_mix[:, l, :], norm_mix[l].rearrange("(k p) -> p k", p=128), w=[g_mix], allow_slow_non_contiguous=True)
        P.dma("sp", g_ffn[:, l, :], norm_ffn[l].rearrange("(k p) -> p k", p=128), w=[g_ffn], allow_slow_non_contiguous=True)
    cwc = cst.alloc([3, 8], F32, "cwc")
    for i in range(3):
        P.dma("sp", cwc[:, i, :], conv_w[i].rearrange("(k p) -> p k", p=128), w=[cwc], allow_slow_non_contiguous=True)
    gq = cst.alloc([1], F32, "gq")
    gk = cst.alloc([1], F32, "gk")
    for hh in range(2):
        P.dma("sp", gq[hh * 64:(hh + 1) * 64, :], q_norm.rearrange("(p o) -> p o", o=1), w=[gq])
        P.dma("sp", gk[hh * 64:(hh + 1) * 64, :], k_norm.rearrange("(p o) -> p o", o=1), w=[gk])
    gq8 = cst.alloc([1], F32, "gq8")
    P.op("dve", lambda h: h.tensor_single_scalar(gq8[:, :], gq[:, :], 0.125, ALU.mult), r=[gq], w=[gq8])
    bias_bc = cst.alloc([8], F32, "bias_bc")
    P.dma("sp", bias_bc[:, :], sb_bias.partition_broadcast(128), w=[bias_bc])
    R = rwkv_setup(P, nc, cst, Wd)
    if cfg.get("dbg", False):
        DBG["ap"] = OUT("dbg", [128, 16384])
        DBG["stage"] = cst.alloc([512], F32, "dbgst")
        DBG["T"] = d_out
        DBG["off"] = 0
        DBG["map"] = {}
    stages = cfg.get("stages", 99)
    padmask = cst.alloc([SPAD], BF16, "padmask")
    P.op("dve", lambda h: h.memset(padmask[:, :], 0.0), w=[padmask])
    P.op("dve", lambda h: h.memset(padmask[:, 0:256].rearrange("p (n t) -> p n t", t=64)[:, :, 0:8], 1.0), w=[padmask])
    Cc = dict(ident_f=ident_f, ident_b=ident_b, ones_b=ones_b, bd_b=bd_b, epsc=epsc, g_mix=g_mix, g_ffn=g_ffn, gq8=gq8, gk=gk, cwc=cwc,
              R=R, padmask=padmask, bias_bc=bias_bc, io_f=io_f, negU=negU, neg1=neg1, maskT=maskT)
    Dm = dict(d_out=d_out, w_in_ab=w_in_ab, w_out_ab=w_out_ab, w_gate=w_gate, w_up=w_up, w_down=w_down, w_in_c=w_in_c, w_out_c=w_out_c)
    Dm["xs_own"] = IN("xs_own", [NS, D])
    Dm["state_wkv"] = IN("state_wkv", [4, 8, 64, 64])
    Dm["state_shift"] = IN("state_shift", [4, PRW])
    Dm["state_conv"] = IN("state_conv", [4, 2, D])
    Dm["y_s"] = OUT("y_s", [NS, D])
    Dm["k_s"] = OUT("k_s", [NS, 512])
    Dm["v_s"] = OUT("v_s", [NS, 512])
    Dm["wkv_s"] = OUT("wkv_s", [4, 8, 64, 64])
    Dm["shift_s"] = OUT("shift_s", [4, PRW])
    Dm["conv_s"] = OUT("conv_s", [4, 2, D])
    if cfg.get("s_attn", True):
        Dm["xs_all"] = IN("xs_all", [256, D])
        Dm["ptab"] = IN("ptab", [32, 128], I32)
        Dm["pt_own"] = IN("pt_own", [4, 128], I32)
        Dm["pids"] = IN("pids", [32, NPG])
        Dm["ck"] = IN("ck", [NPG, 128, 512])
        Dm["cv"] = IN("cv", [NPG, 128, 512])
        Dm["qtok_d"] = dram(nc, "qtok_d", [256, 512], BF16, "Internal")
        Dm["osc"] = dram(nc, "osc", [4, 4096], F32, "Internal")
        Dm["tab_loc"] = nc.dram_tensor("tab_loc", [NPG, TABW], F32).ap()
        Dm["tab_full"] = nc.dram_tensor("tab_full", [8 * NPG, TABW], F32).ap()
    Dm["prw_s"] = dram(nc, "prw_s", [14, 128, SPAD], F32, "Internal")
    Dm["d_prws"] = T(None, "prws")

    for s in range(NSEQ):
        A.mark()
        xst = [A.alloc([D], F32, "xst%d" % i) for i in range(2)]
        for tb in range(16):
            st = xst[tb % 2]
            P.dma("sp", st[:, :], xp[s, tb * 128:(tb + 1) * 128, :], w=[st])
            for half in range(2):
                pb = ps[(tb * 2 + half) % 2]
                for kk in range(4):
                    k = half * 4 + kk
                    P.op("pe", lambda h, pb=pb, kk=kk, st=st, k=k: h.transpose(pb[:, kk * 128:(kk + 1) * 128], st[:, k * 128:(k + 1) * 128], ident_f[:, :]),
                         r=[st, ident_f], w=[pb], skip_self=True)
                xb = xT[tb // 4]
                dst = xT_t[:, half * 4:half * 4 + 4, tb * 128:(tb + 1) * 128]
                src = pb[:, :].rearrange("p (a b) -> p a b", a=4)
                if half == 0:
                    P.op("act", lambda h, dst=dst, src=src: h.copy(dst, src), r=[pb], w=[xb])
                else:
                    P.op("dve", lambda h, dst=dst, src=src: h.tensor_copy(dst, src), r=[pb], w=[xb])
        A.release()
        A.mark()
        qT = A.alloc([4, S], BF16, "qT")
        kT = A.alloc([4, S], BF16, "kT")
        Vt = A.alloc([16, 512], BF16, "Vt")
        A.mark()
        hT = [A.alloc([8, 512], BF16, "hT%d" % b) for b in range(4)]
        rmsnorm_T(P, A, ps, xT_t, xT, hT, g_mix, 0, ones_b, epsc)
        proj_ab(P, A, ps, hT, qT, kT, Vt, w_in_ab, gq8, gk, bd_b, epsc, ident_f, k_p[s], v_p[s], shift_p, s, prw_d, d_prw, d_out, cfg)
        A.release()
        oT = A.alloc([4, S], BF16, "oT")
        sb_attn_prompt(P, A, ps, qT, kT, Vt, oT, bias_bc, negU, neg1, maskT)
        add_proj(P, A, ps, xT_t, xT, oT, 4, w_out_ab[0:512, :])
        A.release()
        A.mark()
        orT = A.alloc([4, S], BF16, "orT")
        if cfg.get("rwkv", True):
            rwkv_seq(P, A, ps, psb, R, prw_d[s], d_prw[s], orT, bd_b, ident_f, ident_b, wkv_p[s], d_out, cfg)
            add_proj(P, A, ps, xT_t, xT, orT, 4, w_out_ab[512:1024, :])
        A.release()
        if stages >= 2:
            ffn(P, A, ps, xT_t, xT, g_ffn, 0, ones_b, epsc, w_gate[0], w_up[0], w_down[0])
        if stages >= 3:
            conv_mix(P, A, ps, xT_t, xT, g_mix, 1, ones_b, epsc, w_in_c, cwc, w_out_c, conv_p[s], d_out)
        if stages >= 4:
            ffn(P, A, ps, xT_t, xT, g_ffn, 1, ones_b, epsc, w_gate[1], w_up[1], w_down[1])
        store_xT(P, A, ps, xT_t, xT, ident_f, y_p[s], d_out)
    if cfg.get("sample", True):
        sample_path(P, A, cst, ps, psb, Cc, Dm, cfg)
    P.finish([d_out] + d_prw + [Dm["d_prws"]] + ([Dm["d_vs"]] if "d_vs" in Dm else []) + Dm.get("d_extra", []))
    with nc.allow_low_precision("bf16 matmul operands"):
        P.emit()


def proj_ab(P, A, ps, hT, qT, kT, Vt, w_in_ab, gq8, gk, bd_b, epsc, ident_f, k_dst, v_dst, shift_p, s, prw_d, d_prw, d_out, cfg):
    A.mark()
    wst = [A.alloc([8, 256], F32, "wst%d" % i) for i in range(2)]
    wbf = [A.alloc([8, 256], BF16, "wbf%d" % i) for i in range(2)]
    sq = [A.alloc([512], BF16, "sq%d" % i) for i in range(2)]
    lnv = [A.alloc([512], F32, "lnv%d" % i) for i in range(2)]
    ev = [A.alloc([512], F32, "ev%d" % i) for i in range(3)]
    tok = [A.alloc([4, 128], F32, "tok%d" % i) for i in range(2)]
    evi = 0
    toki = 0
    npiece = cfg.get("npiece", 13)
    for pc in range(npiece):
        ws, wb = wst[pc % 2], wbf[pc % 2]
        P.dma("sp", ws[:, :, :], w_in_ab[:, pc * 256:(pc + 1) * 256].rearrange("(k p) c -> p k c", p=128), w=[ws])
        ceng = "pool" if pc % 2 == 0 else "dve"
        P.op(ceng, lambda h, ws=ws, wb=wb: h.tensor_copy(wb[:, :, :], ws[:, :, :]), r=[ws], w=[wb])
        for cc in range(2):
            c = pc * 2 + cc
            for b in range(4):
                pm = ps[(c * 4 + b) % 4]
                for k in range(8):
                    P.op("pe", lambda h, pm=pm, wb=wb, k=k, cc=cc, b=b: h.matmul(pm[:, :], lhsT=wb[:, k, cc * 128:(cc + 1) * 128], rhs=hT[b][:, k, :], start=(k == 0), stop=(k == 7)),
                         r=[wb, hT[b]], w=[pm], skip_self=True)
                e = ev[evi % 3]
                evi += 1
                if c < 8:
                    sqt, lv = sq[b % 2], lnv[b % 2]
                    p2 = ps[4 + (b % 2)]
                    P.op("act", lambda h, sqt=sqt, pm=pm: h.activation(sqt[:, :], pm[:, :], AF.Square), r=[pm], w=[sqt])
                    P.op("pe", lambda h, p2=p2, sqt=sqt: h.matmul(p2[:, :], lhsT=bd_b[:, :], rhs=sqt[:, :], start=True, stop=True), r=[sqt, bd_b], w=[p2], skip_self=True)
                    P.op("act", lambda h, lv=lv, p2=p2: h.activation(lv[:, :], p2[:, :], AF.Ln, bias=epsc[:, :], scale=1.0 / 64), r=[p2, epsc], w=[lv])
                    P.op("act", lambda h, lv=lv: h.activation(lv[:, :], lv[:, :], AF.Exp, scale=-0.5), r=[lv], w=[lv])
                    if c < 4:
                        dst = qT[:, c, b * 512:(b + 1) * 512]
                        P.op("dve", lambda h, dst=dst, pm=pm, lv=lv: h.scalar_tensor_tensor(dst, pm[:, :], gq8[:, :], lv[:, :], ALU.mult, ALU.mult), r=[pm, lv, gq8], w=[qT])
                        continue
                    P.op("dve", lambda h, e=e, pm=pm, lv=lv: h.scalar_tensor_tensor(e[:, :], pm[:, :], gk[:, :], lv[:, :], ALU.mult, ALU.mult), r=[pm, lv, gk], w=[e])
                    dst = kT[:, c - 4, b * 512:(b + 1) * 512]
                    P.op("pool", lambda h, dst=dst, e=e: h.tensor_copy(dst, e[:, :]), r=[e], w=[kT])
                else:
                    if (c + b) % 2 == 0:
                        P.op("act", lambda h, e=e, pm=pm: h.copy(e[:, :], pm[:, :]), r=[pm], w=[e])
                    else:
                        P.op("dve", lambda h, e=e, pm=pm: h.tensor_copy(e[:, :], pm[:, :]), r=[pm], w=[e])
                if c >= 12:
                    P.dma("pool", prw_d[s, c - 12, :, b * 512:(b + 1) * 512], e[:, :], r=[e], w=[d_prw[s]])
                    if b == 3:
                        cc0 = (c - 12) * 128
                        P.dma("pool", shift_p[s:s + 1, cc0:cc0 + 128].rearrange("o p -> p o"), e[:, 511:512], r=[e], w=[d_out])
                    continue
                pt = ps[6 + (toki % 2)]
                tk = tok[toki % 2]
                toki += 1
                for j in range(4):
                    P.op("pe", lambda h, pt=pt, e=e, j=j: h.transpose(pt[:, j * 128:(j + 1) * 128], e[:, j * 128:(j + 1) * 128], ident_f[:, :]), r=[e, ident_f], w=[pt], skip_self=True)
                P.op("act", lambda h, tk=tk, pt=pt: h.copy(tk[:, :, :], pt[:, :].rearrange("p (a b) -> p a b", a=4)), r=[pt], w=[tk])
                hp = (c - 4) % 4
                dd = k_dst if c < 8 else v_dst
                P.dma("pool", dd[b * 512:(b + 1) * 512, hp * 128:(hp + 1) * 128].rearrange("(a p) c -> p a c", p=128), tk[:, :, :], r=[tk], w=[d_out])
                if c >= 8:
                    P.op("dve", lambda h, tk=tk, hp=hp, b=b: h.tensor_copy(Vt[:, b * 4:(b + 1) * 4, hp * 128:(hp + 1) * 128], tk[:, :, :]), r=[tk], w=[Vt])
    A.release()


def rmsnorm_T(P, A, ps, xT_t, xT, hT, g, gl, ones_b, epsc, blocks=None):
    A.mark()
    sq = [A.alloc([8, 512], BF16, "nsq%d" % i) for i in range(2)]
    rs = [A.alloc([512], F32, "nrs%d" % i) for i in range(2)]
    if blocks is None:
        blocks = [0, 1, 2, 3]
    for li, b in enumerate(blocks):
        sqt, r_ = sq[b % 2], rs[b % 2]
        pm = ps[4 + b % 2]
        src = xT_t[:, :, b * 512:(b + 1) * 512]
        P.op("act", lambda h, sqt=sqt, src=src: h.activation(sqt[:, :, :], src, AF.Square), r=[xT[b]], w=[sqt])
        for k in range(8):
            P.op("pe", lambda h, pm=pm, sqt=sqt, k=k: h.matmul(pm[:, :], lhsT=ones_b[:, :], rhs=sqt[:, k, :], start=(k == 0), stop=(k == 7)), r=[sqt, ones_b], w=[pm], skip_self=True)
        P.op("act", lambda h, r_=r_, pm=pm: h.activation(r_[:, :], pm[:, :], AF.Ln, bias=epsc[:, :], scale=1.0 / 1024), r=[pm, epsc], w=[r_])
        P.op("act", lambda h, r_=r_: h.activation(r_[:, :], r_[:, :], AF.Exp, scale=-0.5), r=[r_], w=[r_])
        for k in range(8):
            eng = "dve" if k % 2 == 0 else "dve"
            P.op(eng, lambda h, b=b, k=k, r_=r_, li=li: h.scalar_tensor_tensor(hT[li][:, k, :], xT_t[:, k, b * 512:(b + 1) * 512], g[:, gl, k:k + 1], r_[:, :], ALU.mult, ALU.mult),
                 r=[xT[b], r_, g], w=[hT[li]])
    A.release()


def sb_attn_prompt(P, A, ps, qT, kT, Vt, oT, bias_bc, negU, neg1, maskT):
    A.mark()
    et = [A.alloc([512], F32, "e%d" % i) for i in range(2)]
    spt = [A.alloc([512], BF16, "sp%d" % i) for i in range(2)]
    At = [A.alloc([512], BF16, "A%d" % i) for i in range(2)]
    Rf = [A.alloc([128], F32, "Rf%d" % i) for i in range(2)]
    Rb = [A.alloc([128], BF16, "Rb%d" % i) for i in range(2)]
    tmp = [A.alloc([128], F32, "rt%d" % i) for i in range(2)]
    u = 0
    for qb in range(16):
        po = ps[6 + qb % 2]
        first = {0: True, 1: True}
        for h_ in range(8):
            hp, hh = h_ // 2, h_ % 2
            p0 = hh * 64
            ngrp = qb // 4 + 1
            rf, rb = Rf[h_ % 2], Rb[h_ % 2]
            for g in range(ngrp - 1, -1, -1):
                kb0 = 4 * g
                nb = min(4, qb + 1 - kb0)
                W = nb * 128
                pz = ps[u % 4]
                e, sp, a = et[u % 2], spt[u % 2], At[u % 2]
                u += 1
                for i in range(nb):
                    kb = kb0 + i
                    P.op("pe", lambda h, pz=pz, i=i, kb=kb, hp=hp, p0=p0, qb=qb: h.matmul(
                        pz[:, i * 128:(i + 1) * 128], lhsT=kT[p0:p0 + 64, hp, kb * 128:(kb + 1) * 128],
                        rhs=qT[p0:p0 + 64, hp, qb * 128:(qb + 1) * 128], start=(i == 0), stop=False, skip_group_check=True),
                        r=[kT, qT], w=[pz], skip_self=True)
                bcol = bias_bc[:, h_:h_ + 1]
                P.op("act", lambda h, e=e, pz=pz, W=W, bcol=bcol: h.activation(e[:, 0:W], pz[:, 0:W], AF.Exp, bias=bcol), r=[pz, bias_bc], w=[e])
                P.op("act", lambda h, e=e, sp=sp, W=W: h.activation(sp[:, 0:W], e[:, 0:W], AF.Ln, bias=1.0), r=[e], w=[sp])
                diag = (g == ngrp - 1)
                if diag:
                    i = nb - 1
                    P.op("pool", lambda h, sp=sp, i=i: h.tensor_tensor(sp[:, i * 128:(i + 1) * 128], sp[:, i * 128:(i + 1) * 128], maskT[:, :], ALU.mult), r=[sp, maskT], w=[sp])
                P.op("pe", lambda h, pz=pz, sp=sp, W=W: h.matmul(pz[:, 0:W], lhsT=negU[:, :], rhs=sp[:, 0:W], start=False, stop=False, skip_group_check=True), r=[sp, negU], w=[pz], skip_self=True)
                for i in range(nb - 1, 0, -1):
                    P.op("pe", lambda h, pz=pz, sp=sp, i=i: h.matmul(pz[:, 0:i * 128].rearrange("p (a b) -> p a b", a=i), lhsT=neg1[:, :],
                                                                  rhs=sp[:, i * 128:(i + 1) * 128].unsqueeze(1).broadcast_to([128, i, 128]), start=False, stop=False, skip_group_check=True),
                         r=[sp, neg1], w=[pz], skip_self=True)
                if not diag:
                    P.op("pe", lambda h, pz=pz, rb=rb, nb=nb: h.matmul(pz[:, 0:nb * 128].rearrange("p (a b) -> p a b", a=nb), lhsT=neg1[:, :],
                                                                   rhs=rb[:, :].unsqueeze(1).broadcast_to([128, nb, 128]), start=False, stop=True, skip_group_check=True),
                         r=[rb, neg1], w=[pz], skip_self=True)
                if g > 0:
                    tm = tmp[u % 2]
                    P.op("dve", lambda h, tm=tm, sp=sp, nb=nb: h.tensor_reduce(tm[:, :], sp[:, 0:nb * 128].rearrange("p (a b) -> p b a", a=nb), AX.X, ALU.add), r=[sp], w=[tm])
                    if diag:
                        P.op("dve", lambda h, rf=rf, tm=tm: h.tensor_copy(rf[:, :], tm[:, :]), r=[tm], w=[rf])
                    else:
                        P.op("dve", lambda h, rf=rf, tm=tm: h.tensor_tensor(rf[:, :], rf[:, :], tm[:, :], ALU.add), r=[tm, rf], w=[rf])
                    P.op("dve", lambda h, rf=rf, rb=rb: h.tensor_copy(rb[:, :], rf[:, :]), r=[rf], w=[rb])
                P.op("act", lambda h, a=a, pz=pz, W=W, bcol=bcol: h.activation(a[:, 0:W], pz[:, 0:W], AF.Exp, bias=bcol), r=[pz, bias_bc], w=[a])
                if diag:
                    i = nb - 1
                    P.op("pool", lambda h, a=a, i=i: h.tensor_tensor(a[:, i * 128:(i + 1) * 128], a[:, i * 128:(i + 1) * 128], maskT[:, :], ALU.mult), r=[a, maskT], w=[a])
                for i in range(nb):
                    kb = kb0 + i
                    st = first[hh]
                    first[hh] = False
                    P.op("pe", lambda h, po=po, kb=kb, h_=h_, a=a, i=i, p0=p0, hp=hp, st=st: h.matmul(
                        po[p0:p0 + 64, hp * 128:(hp + 1) * 128], lhsT=Vt[:, kb, h_ * 64:(h_ + 1) * 64], rhs=a[:, i * 128:(i + 1) * 128],
                        start=st, stop=False, skip_group_check=True), r=[Vt, a], w=[po], skip_self=True)
        dst = oT[:, :, qb * 128:(qb + 1) * 128]
        P.op("dve", lambda h, dst=dst, po=po: h.tensor_copy(dst, po[:, :].rearrange("p (a b) -> p a b", a=4)), r=[po], w=[oT])
    A.release()

C_DEC = 0.6065306597126334
GN_EPS = 64e-5


def load_col(P, cst, vec, n, name):
    t = cst.alloc([n], F32, name)
    P.dma("sp", t[:, :], vec.rearrange("(c p) -> p c", p=128), w=[t], allow_slow_non_contiguous=True)
    return t


def rwkv_setup(P, nc, cst, Wd):
    R = {}
    R["mu"] = load_col(P, cst, Wd["mu_rw"], 14, "mu")
    for nm in ["w0", "a0", "k_k", "k_a", "r_k", "lnx_w", "lnx_b"]:
        R[nm] = load_col(P, cst, Wd[nm], 4, nm)
    omka = cst.alloc([4], F32, "omka")
    P.op("dve", lambda h: h.tensor_scalar(omka[:, :], R["k_a"][:, :], -1.0, 1.0, ALU.mult, ALU.add), r=[R["k_a"]], w=[omka])
    R["omka"] = omka
    st = cst.alloc([512], F32, "lst")
    w2a2 = cst.alloc([512], BF16, "w2a2")
    P.dma("sp", st[0:64, :], Wd["w2"], w=[st])
    P.dma("sp", st[64:128, :], Wd["a2"], w=[st])
    P.op("dve", lambda h: h.tensor_copy(w2a2[:, :], st[:, :]), r=[st], w=[w2a2])
    g2b = cst.alloc([512], BF16, "g2b")
    P.dma("sp", st[:, :], Wd["g2"], w=[st])
    P.op("dve", lambda h: h.tensor_copy(g2b[:, :], st[:, :]), r=[st], w=[g2b])
    R["w2a2"], R["g2b"] = w2a2, g2b
    gne = cst.alloc([1], F32, "gne")
    P.op("dve", lambda h: h.memset(gne[:, :], GN_EPS), w=[gne])
    R["gne"] = gne
    i64 = cst.alloc([64], I32, "i64")
    for hh in range(2):
        P.op("pool", lambda h, hh=hh: h.iota(i64[hh * 64:(hh + 1) * 64, :], pattern=[[1, 64]], base=0, channel_multiplier=-1), w=[i64])
    f64 = cst.alloc([64], F32, "f64")
    P.op("dve", lambda h: h.tensor_copy(f64[:, :], i64[:, :]), r=[i64], w=[f64])
    m4 = cst.alloc([4, 64], BF16, "m4")
    for k in range(4):
        opk = ALU.is_gt if k % 2 == 0 else ALU.is_ge
        P.op("dve", lambda h, k=k, opk=opk: h.tensor_single_scalar(m4[:, k, :], f64[:, :], 0.0, opk), r=[f64], w=[m4])
    mL = cst.alloc([64], BF16, "mL")
    P.op("dve", lambda h: h.tensor_single_scalar(mL[:, :], f64[:, :], 0.0, ALU.is_lt), r=[f64], w=[mL])
    i64b = cst.alloc([64], BF16, "i64b")
    P.op("dve", lambda h: h.tensor_single_scalar(i64b[:, :], f64[:, :], 0.0, ALU.is_equal), r=[f64], w=[i64b])
    rmask = cst.alloc([S], BF16, "rmask")
    P.op("dve", lambda h: h.memset(rmask[:, :], 1.0), w=[rmask])
    P.op("dve", lambda h: h.memset(rmask[:, :].rearrange("p (n t) -> p n t", t=64)[:, :, 0:1], 0.0), w=[rmask])
    R.update(m4=m4, mL=mL, i64b=i64b, rmask=rmask)
    return R


DBG = {"ap": None, "off": 0, "map": {}, "stage": None, "T": None}


def dbg_dump(P, name, ap, ncols, tiles, parts=128):
    if DBG["ap"] is None:
        return
    st = DBG["stage"]
    off = DBG["off"]
    for c0 in range(0, ncols, 512):
        cc = min(512, ncols - c0)
        P.op("dve", lambda h, c0=c0, cc=cc: h.tensor_copy(st[0:parts, 0:cc], ap[:, c0:c0 + cc]), r=list(tiles), w=[st])
        P.dma("pool", DBG["ap"][0:parts, off + c0:off + c0 + cc], st[0:parts, 0:cc], r=[st], w=[DBG["T"]])
    DBG["map"][name] = (off, ncols)
    DBG["off"] = off + ncols


def shift_mix(P, buf, dd, dst, mucol, NT, extra_r=()):
    P.op("dve", lambda h: h.tensor_tensor(dd[:, 0:NT], buf[:, 0:NT], buf[:, 1:NT + 1], ALU.subtract), r=[buf], w=[dd])
    P.op("dve", lambda h: h.scalar_tensor_tensor(dst[:, 0:NT], dd[:, 0:NT], mucol, buf[:, 1:NT + 1], ALU.mult, ALU.add), r=[buf, dd] + list(extra_r), w=[dst])


def prev_cols(P, buf, ci, sample):
    if sample is None:
        P.op("pool", lambda h: h.memset(buf[:, 0:1], 0.0), w=[buf])
    else:
        for b in range(4):
            P.dma("sp", buf[:, 64 * b:64 * b + 1], sample["state_shift"][b:b + 1, ci * 128:(ci + 1) * 128].rearrange("o p -> p o"), w=[buf])


def rwkv_seq(P, A, ps, psb, R, prw_src, d_prw, orT, bd_b, ident_f, ident_b, wkv_dst, d_out, cfg={}, NT=S, sample=None):
    NCH = NT // 64
    NBK = NT // 128
    mu = R["mu"]
    A.mark()
    tw = A.alloc([NT], BF16, "tw")
    sg = A.alloc([NT], BF16, "sg")
    A.mark()
    buf = A.alloc([NT + 1], F32, "lbuf")
    dd = A.alloc([NT], F32, "ldd")
    xm = A.alloc([NT], F32, "lxm")
    for ci in (12, 13):
        P.dma("sp", buf[:, 1:NT + 1], prw_src[ci], r=[d_prw], w=[buf])
        prev_cols(P, buf, ci, sample)
        shift_mix(P, buf, dd, xm, mu[:, ci:ci + 1], NT, extra_r=[mu])
        if ci == 12:
            P.op("act", lambda h: h.activation(tw[0:64, :], xm[0:64, :], AF.Tanh), r=[xm], w=[tw])
            P.op("act", lambda h: h.copy(tw[64:128, :], xm[64:128, :]), r=[xm], w=[tw])
        else:
            P.op("act", lambda h: h.activation(sg[:, :], xm[:, :], AF.Sigmoid), r=[xm], w=[sg])
    A.release()
    def do_hp(hp):
        A.mark()
        prod = A.alloc([NT], BF16, "prod")
        KR = A.alloc([NCH, 2, 64], BF16, "KR")
        KhT = A.alloc([NT], BF16, "KhT")
        BhT = A.alloc([NT], BF16, "BhT")
        PC = A.alloc([NCH], F32, "PC")
        V = A.alloc([NT], F32, "V")
        A.mark()
        buf = A.alloc([NT + 1], F32, "buf")
        dd = A.alloc([NT], F32, "dd")
        Rr = A.alloc([NT], F32, "Rr")
        KX = A.alloc([NT], F32, "KX")
        av = A.alloc([NT], F32, "av")
        sig = A.alloc([NT], F32, "sig")
        kk = A.alloc([NT], F32, "kk")
        cum = A.alloc([NT], F32, "cum")
        for ci, dst in [(hp, Rr), (4 + hp, KX), (8 + hp, V)]:
            P.dma("sp", buf[:, 1:NT + 1], prw_src[ci], r=[d_prw], w=[buf])
            prev_cols(P, buf, ci, sample)
            shift_mix(P, buf, dd, dst, mu[:, ci:ci + 1], NT, extra_r=[mu])
            if sample is not None:
                P.op("dve", lambda h, dst=dst: h.tensor_tensor(dst[:, :], dst[:, :], sample["padmask"][:, :], ALU.mult), r=[dst, sample["padmask"]], w=[dst])
        for b in range(NT // 512):
            sl = slice(b * 512, (b + 1) * 512)
            p1, p2 = ps[b % 2], ps[2 + b % 2]
            P.op("pe", lambda h, p1=p1, sl=sl: h.matmul(p1[:, :], lhsT=R["w2a2"][0:64, hp * 128:(hp + 1) * 128], rhs=tw[0:64, sl], start=True, stop=True), r=[R["w2a2"], tw], w=[p1], skip_self=True)
            P.op("pe", lambda h, p2=p2, sl=sl: h.matmul(p2[:, :], lhsT=R["w2a2"][64:128, hp * 128:(hp + 1) * 128], rhs=tw[64:128, sl], start=True, stop=True), r=[R["w2a2"], tw], w=[p2], skip_self=True)
            P.op("act", lambda h, p1=p1, sl=sl: h.activation(sig[:, sl], p1[:, :], AF.Sigmoid, bias=R["w0"][:, hp:hp + 1]), r=[p1, R["w0"]], w=[sig])
            P.op("act", lambda h, p2=p2, sl=sl: h.activation(av[:, sl], p2[:, :], AF.Sigmoid, bias=R["a0"][:, hp:hp + 1]), r=[p2, R["a0"]], w=[av])
        if sample is not None:
            P.op("dve", lambda h: h.tensor_tensor(sig[:, :], sig[:, :], sample["padmask"][:, :], ALU.mult), r=[sig, sample["padmask"]], w=[sig])
        P.op("dve", lambda h: h.tensor_scalar(kk[:, :], KX[:, :], R["k_k"][:, hp:hp + 1], None, ALU.mult), r=[KX, R["k_k"]], w=[kk])
        sqb = A.alloc([512], BF16, "sqb")
        rin = A.alloc([512], F32, "rin")
        for b in range(NT // 512):
            sl = slice(b * 512, (b + 1) * 512)
            p1 = ps[4 + b % 2]
            P.op("act", lambda h, sl=sl: h.activation(sqb[:, :], kk[:, sl], AF.Square), r=[kk], w=[sqb])
            P.op("pe", lambda h, p1=p1: h.matmul(p1[:, :], lhsT=bd_b[:, :], rhs=sqb[:, :], start=True, stop=True), r=[sqb, bd_b], w=[p1], skip_self=True)
            P.op("dve", lambda h, p1=p1: h.tensor_scalar(rin[:, :], p1[:, :], 1e-24, None, ALU.max), r=[p1], w=[rin])
            P.op("act", lambda h: h.activation(rin[:, :], rin[:, :], AF.Ln), r=[rin], w=[rin])
            P.op("act", lambda h: h.activation(rin[:, :], rin[:, :], AF.Exp, scale=-0.5), r=[rin], w=[rin])
            P.op("dve", lambda h, sl=sl: h.tensor_tensor(kk[:, sl], kk[:, sl], rin[:, :], ALU.mult), r=[kk, rin], w=[kk])
        P.op("dve", lambda h: h.tensor_scalar(dd[:, :], av[:, :], R["k_a"][:, hp:hp + 1], R["omka"][:, hp:hp + 1], ALU.mult, ALU.add), r=[av, R["k_a"], R["omka"]], w=[dd])
        P.op("dve", lambda h: h.tensor_tensor(KX[:, :], KX[:, :], dd[:, :], ALU.mult), r=[KX, dd], w=[KX])
        P.op("dve", lambda h: h.scalar_tensor_tensor(prod[:, :], Rr[:, :], R["r_k"][:, hp:hp + 1], KX[:, :], ALU.mult, ALU.mult), r=[Rr, KX, R["r_k"]], w=[prod])
        P.op("pool", lambda h: h.tensor_tensor(av[:, :], av[:, :], kk[:, :], ALU.mult), r=[av, kk], w=[av])
        P.op("dve", lambda h: h.tensor_tensor_scan(cum[:, :], R["rmask"][:, 0:NT], sig[:, :], 0.0, ALU.mult, ALU.add), r=[R["rmask"], sig], w=[cum])
        v3 = lambda t: t[:, :].rearrange("p (n t) -> p n t", t=64)
        if hp == 0 and sample is None:
            dbg_dump(P, "sig", sig[:, :], 512, [sig])
            dbg_dump(P, "cum", cum[:, :], 512, [cum])
            dbg_dump(P, "kk", kk[:, :], 512, [kk])
            dbg_dump(P, "Rr", Rr[:, :], 512, [Rr])
            dbg_dump(P, "KX", KX[:, :], 512, [KX])
            dbg_dump(P, "tw", tw[:, :], 512, [tw])
        P.op("act", lambda h: h.activation(dd[:, :], cum[:, :], AF.Exp, scale=-C_DEC), r=[cum], w=[dd])
        P.op("dve", lambda h: h.tensor_tensor(KR[:, :, 1, :], v3(Rr), v3(dd), ALU.mult), r=[Rr, dd], w=[KR])
        P.op("pool", lambda h: h.tensor_copy(PC[:, :], v3(dd)[:, :, 63]), r=[dd], w=[PC])
        P.op("act", lambda h: h.activation(dd[:, :], cum[:, :], AF.Exp, scale=C_DEC), r=[cum], w=[dd])
        P.op("dve", lambda h: h.tensor_tensor(KhT[:, :], KX[:, :], dd[:, :], ALU.mult), r=[KX, dd], w=[KhT])
        P.op("pool", lambda h: h.tensor_tensor(BhT[:, :], av[:, :], dd[:, :], ALU.mult), r=[av, dd], w=[BhT])
        P.op("dve", lambda h: h.tensor_tensor(cum[:, :], cum[:, :], sig[:, :], ALU.subtract), r=[cum, sig], w=[cum])
        P.op("act", lambda h: h.activation(dd[:, :], cum[:, :], AF.Exp, scale=-C_DEC), r=[cum], w=[dd])
        P.op("dve", lambda h: h.tensor_tensor(KR[:, :, 0, :], v3(kk), v3(dd), ALU.mult), r=[kk, dd], w=[KR])
        if hp == 0 and sample is None:
            dbg_dump(P, "KR", KR[:, :, :, :].rearrange("p n k t -> p (n k t)"), NCH * 128, [KR])
            dbg_dump(P, "KhT", KhT[:, :], NT, [KhT])
            dbg_dump(P, "BhT", BhT[:, :], NT, [BhT])
            dbg_dump(P, "PC", PC[:, :], NCH, [PC])
        A.release()
        if cfg.get("rw_phase", 9) <= 1:
            A.release()
            return
        A.mark()
        Vb = A.alloc([NT], BF16, "Vb")
        Ktok = A.alloc([NCH, 64], BF16, "Ktok")
        Btok = A.alloc([NCH, 64], BF16, "Btok")
        Vtok = A.alloc([NCH, 64], BF16, "Vtok")
        Aall = A.alloc([NCH, 4, 64], BF16, "Aall")
        NG = NCH // 8
        Acur = [A.alloc([8, 64], BF16, "Acur%d" % g) for g in range(NG)]
        Xcur = [A.alloc([8, 64], BF16, "Xcur%d" % g) for g in range(NG)]
        Anew = [A.alloc([8, 64], BF16, "Anew%d" % g) for g in range(NG)]
        Xnew = [A.alloc([8, 64], BF16, "Xnew%d" % g) for g in range(NG)]
        Mt = [A.alloc([8, 64], BF16, "Mt%d" % g) for g in range(NG)]
        yT = A.alloc([NT], F32, "yT")
        P.op("pool", lambda h: h.tensor_copy(Vb[:, :], V[:, :]), r=[V], w=[Vb])
        ti = 0
        for src, dst in [(KhT, Ktok), (BhT, Btok), (Vb, Vtok)]:
            for g16 in range(NCH // 16):
                pb = 6 + ti % 2
                ti += 1
                for i in range(16):
                    n = g16 * 16 + i
                    for hh in range(2):
                        p0 = 64 * hh
                        P.op("pe", lambda h, pb=pb, i=i, n=n, src=src, p0=p0: h.transpose(psb[pb][p0:p0 + 64, i * 64:(i + 1) * 64], src[p0:p0 + 64, n * 64:(n + 1) * 64], ident_b[p0:p0 + 64, p0:p0 + 64]), r=[src, ident_b], w=[ps[pb]], skip_self=True)
                d_ = dst[:, g16 * 16:(g16 + 1) * 16, :]
                s_ = psb[pb][:, :].rearrange("p (a b) -> p a b", a=16)
                if ti % 2 == 0:
                    P.op("act", lambda h, d_=d_, s_=s_: h.copy(d_, s_), r=[ps[pb]], w=[dst])
                else:
                    P.op("dve", lambda h, d_=d_, s_=s_: h.tensor_copy(d_, s_), r=[ps[pb]], w=[dst])
        if cfg.get("rw_phase", 9) <= 2:
            A.release()
            A.release()
            return
        for n in range(NCH):
            g, a = n // 8, n % 8
            pa = ps[(n // 2) % 4]
            c0 = (n % 2) * 256
            for hh in range(2):
                p0 = 64 * hh
                for k2, lT in [(0, KhT), (1, BhT)]:
                    P.op("pe", lambda h, pa=pa, c0=c0, k2=k2, lT=lT, p0=p0, n=n: h.matmul(
                        pa[p0:p0 + 64, c0 + k2 * 128: c0 + (k2 + 1) * 128], lhsT=lT[p0:p0 + 64, n * 64:(n + 1) * 64],
                        rhs=KR[p0:p0 + 64, n, :, :], start=(n % 2 == 0 and k2 == 0), stop=False, skip_group_check=True), r=[lT, KR], w=[pa], skip_self=True)
            if n % 2 == 1:
                dstA = Aall[:, n - 1:n + 1, :, :]
                srcA = pa[:, :].rearrange("p (n k t) -> p n k t", n=2, k=4)
                mk = R["m4"][:, :, :].unsqueeze(1).broadcast_to([128, 2, 4, 64])
                P.op("dve", lambda h, dstA=dstA, srcA=srcA, mk=mk: h.tensor_tensor(dstA, srcA, mk, ALU.mult), r=[pa, R["m4"]], w=[Aall])
            if cfg.get("rw_phase", 9) <= 2.5:
                continue
            pc_ = ps[4 + g % 2]
            for hh in range(2):
                p0 = 64 * hh
                P.op("pe", lambda h, pc_=pc_, a=a, p0=p0, n=n: h.matmul(
                    pc_[p0:p0 + 64, a * 64:(a + 1) * 64], lhsT=KR[p0:p0 + 64, n, 0, :], rhs=BhT[p0:p0 + 64, n * 64:(n + 1) * 64],
                    start=(a == 0), stop=False, skip_group_check=True), r=[KR, BhT], w=[pc_], skip_self=True)
            if a == 7 and cfg.get("rw_phase", 9) > 2.7:
                srcL = pc_[:, :].rearrange("p (a t) -> p a t", t=64)
                mkL = R["mL"][:, :].unsqueeze(1).broadcast_to([128, 8, 64])
                xc, ac, an, mt = Xcur[g], Acur[g], Anew[g], Mt[g]
                P.op("act", lambda h, an=an, srcL=srcL: h.copy(an[:, :, :], srcL), r=[pc_], w=[an])
                P.op("pool", lambda h, an=an, ac=ac, mkL=mkL: h.tensor_tensor(ac[:, :, :], an[:, :, :], mkL, ALU.mult), r=[an, R["mL"]], w=[ac])
                x0 = Aall[:, g * 8:(g + 1) * 8, 2, :]
                P.op("pool", lambda h, xc=xc, x0=x0: h.tensor_copy(xc[:, :, :], x0), r=[Aall], w=[xc])
                idb = R["i64b"][:, :].unsqueeze(1).broadcast_to([128, 8, 64])
                P.op("dve", lambda h, mt=mt, xc=xc, idb=idb: h.tensor_tensor(mt[:, :, :], idb, xc[:, :, :], ALU.subtract), r=[xc, R["i64b"]], w=[mt])
        if hp == 0 and sample is None:
            dbg_dump(P, "Aall0", Aall[:, 0:2, :, :].rearrange("p n k t -> p (n k t)"), 512, [Aall])
            dbg_dump(P, "Acur0", Acur[0][:, :, :].rearrange("p a t -> p (a t)"), 512, [Acur[0]])
            dbg_dump(P, "Mt_init", Mt[0][:, :, :].rearrange("p a t -> p (a t)"), 512, [Mt[0]])
            dbg_dump(P, "Ktok", Ktok[:, 0:4, :].rearrange("p a t -> p (a t)"), 256, [Ktok])
            dbg_dump(P, "Vtok", Vtok[:, 0:4, :].rearrange("p a t -> p (a t)"), 256, [Vtok])
        if cfg.get("rw_phase", 9) <= 3:
            A.release()
            A.release()
            return
        fl = lambda t: t[:, :, :].rearrange("p a t -> p (a t)")
        for lvl in range(1, 6):
            for g in range(NG):
                ba, bb, bc = ps[(g % 2) * 3 + 0], ps[(g % 2) * 3 + 1], ps[(g % 2) * 3 + 2]
                xc, ac, an, xn, mt = Xcur[g], Acur[g], Anew[g], Xnew[g], Mt[g]
                units = [(a, hh) for a in range(8) for hh in range(2)]
                for (a, hh) in units:
                    p0 = 64 * hh
                    P.op("pe", lambda h, ba=ba, p0=p0, a=a, xc=xc, ac=ac: h.matmul(ba[p0:p0 + 64, a * 64:(a + 1) * 64], lhsT=xc[p0:p0 + 64, a, :], rhs=ac[p0:p0 + 64, a, :], start=(a == 0), stop=False, skip_group_check=True),
                         r=[xc, ac], w=[ba], skip_self=True)
                if lvl < 5:
                    for (a, hh) in units:
                        p0 = 64 * hh
                        P.op("pe", lambda h, bb=bb, p0=p0, a=a, xc=xc, ac=ac: h.matmul(bb[p0:p0 + 64, a * 64:(a + 1) * 64], lhsT=ac[p0:p0 + 64, a, :], rhs=xc[p0:p0 + 64, a, :], start=(a == 0), stop=False, skip_group_check=True),
                             r=[xc, ac], w=[bb], skip_self=True)
                P.op("act", lambda h, an=an, ba=ba: h.copy(fl(an), ba[:, :]), r=[ba], w=[an])
                if lvl < 5:
                    P.op("dve", lambda h, xn=xn, bb=bb: h.tensor_copy(fl(xn), bb[:, :]), r=[bb], w=[xn])
                for (a, hh) in units:
                    p0 = 64 * hh
                    P.op("pe", lambda h, bc=bc, p0=p0, a=a, an=an, mt=mt: h.matmul(bc[p0:p0 + 64, a * 64:(a + 1) * 64], lhsT=an[p0:p0 + 64, a, :], rhs=mt[p0:p0 + 64, a, :], start=(a == 0), stop=False, skip_group_check=True),
                         r=[an, mt], w=[bc], skip_self=True)
                P.op("dve", lambda h, mt=mt, bc=bc: h.tensor_tensor(fl(mt), bc[:, :], fl(mt), ALU.add), r=[bc, mt], w=[mt])
                Acur[g], Anew[g] = Anew[g], Acur[g]
                if lvl < 5:
                    Xcur[g], Xnew[g] = Xnew[g], Xcur[g]
        if hp == 0 and sample is None:
            dbg_dump(P, "Mt_fin", Mt[0][:, :, :].rearrange("p a t -> p (a t)"), 512, [Mt[0]])
        if cfg.get("rw_phase", 9) <= 4:
            A.release()
            A.release()
            return
        Tf = A.alloc([64], F32, "Tf")
        Tb = [A.alloc([64], BF16, "Tb%d" % i) for i in range(2)]
        tmpT = A.alloc([64], F32, "tmpT")
        Zb = [A.alloc([64], BF16, "Zb%d" % i) for i in range(2)]
        Ub = [A.alloc([64], BF16, "Ub%d" % i) for i in range(2)]
        P.op("dve", lambda h: h.memset(Tf[:, :], 0.0), w=[Tf])
        P.op("dve", lambda h: h.memset(Tb[0][:, :], 0.0), w=[Tb[0]])
        sw = A.alloc([128], F32, "sw") if sample is not None else None
        wk = A.alloc([128], F32, "wk")
        for n in range(NCH if sample is None else 8):
            g, a = n // 8, n % 8
            Zp, Up, Tp = ps[n % 2], ps[2 + n % 2], ps[4 + n % 2]
            Yp = ps[6 + (n // 8) % 2]
            To, Tn = Tb[n % 2], Tb[(n + 1) % 2]
            zb, ub = Zb[n % 2], Ub[n % 2]
            c8 = n % 8
            if sample is not None and n < 4:
                for hh in range(2):
                    P.dma("sp", Tf[hh * 64:(hh + 1) * 64, :], sample["state_wkv"][n, 2 * hp + hh].rearrange("i j -> j i"), w=[Tf], allow_slow_non_contiguous=True)
                P.op("act", lambda h, To=To: h.copy(To[:, :], Tf[:, :]), r=[Tf], w=[To])
            for hh in range(2):
                p0 = 64 * hh
                P.op("pe", lambda h, Zp=Zp, p0=p0, n=n, To=To: h.matmul(Zp[p0:p0 + 64, 0:64], lhsT=KR[p0:p0 + 64, n, 0, :], rhs=To[p0:p0 + 64, :], start=True, stop=False, skip_group_check=True),
                     r=[KR, To], w=[Zp], skip_self=True)
                P.op("pe", lambda h, Zp=Zp, p0=p0, n=n: h.matmul(Zp[p0:p0 + 64, 0:64], lhsT=Aall[p0:p0 + 64, n, 0, :], rhs=Vtok[p0:p0 + 64, n, :], start=False, stop=True, skip_group_check=True),
                     r=[Aall, Vtok], w=[Zp], skip_self=True)
            P.op("act", lambda h, zb=zb, Zp=Zp: h.copy(zb[:, :], Zp[:, 0:64]), r=[Zp], w=[zb])
            for hh in range(2):
                p0 = 64 * hh
                P.op("pe", lambda h, Up=Up, p0=p0, g=g, a=a, zb=zb: h.matmul(Up[p0:p0 + 64, 0:64], lhsT=Mt[g][p0:p0 + 64, a, :], rhs=zb[p0:p0 + 64, :], start=True, stop=True, skip_group_check=True),
                     r=[Mt[g], zb], w=[Up], skip_self=True)
            P.op("act", lambda h, ub=ub, Up=Up: h.mul(ub[:, :], Up[:, 0:64], -1.0), r=[Up], w=[ub])
            for hh in range(2):
                p0 = 64 * hh
                cs = slice(c8 * 64, (c8 + 1) * 64)
                P.op("pe", lambda h, Yp=Yp, p0=p0, n=n, To=To, cs=cs, c8=c8: h.matmul(Yp[p0:p0 + 64, cs], lhsT=To[p0:p0 + 64, :], rhs=KR[p0:p0 + 64, n, 1, :], start=(c8 == 0), stop=False, skip_group_check=True),
                     r=[KR, To], w=[Yp], skip_self=True)
                P.op("pe", lambda h, Yp=Yp, p0=p0, n=n, cs=cs: h.matmul(Yp[p0:p0 + 64, cs], lhsT=Vtok[p0:p0 + 64, n, :], rhs=Aall[p0:p0 + 64, n, 1, :], start=False, stop=False, skip_group_check=True),
                     r=[Aall, Vtok], w=[Yp], skip_self=True)
                P.op("pe", lambda h, Yp=Yp, p0=p0, n=n, cs=cs, ub=ub: h.matmul(Yp[p0:p0 + 64, cs], lhsT=ub[p0:p0 + 64, :], rhs=Aall[p0:p0 + 64, n, 3, :], start=False, stop=False, skip_group_check=True),
                     r=[Aall, ub], w=[Yp], skip_self=True)
            for hh in range(2):
                p0 = 64 * hh
                P.op("pe", lambda h, Tp=Tp, p0=p0, n=n: h.matmul(Tp[p0:p0 + 64, 0:64], lhsT=Ktok[p0:p0 + 64, n, :], rhs=Vtok[p0:p0 + 64, n, :], start=True, stop=False, skip_group_check=True),
                     r=[Ktok, Vtok], w=[Tp], skip_self=True)
                P.op("pe", lambda h, Tp=Tp, p0=p0, n=n, ub=ub: h.matmul(Tp[p0:p0 + 64, 0:64], lhsT=Btok[p0:p0 + 64, n, :], rhs=ub[p0:p0 + 64, :], start=False, stop=True, skip_group_check=True),
                     r=[Btok, ub], w=[Tp], skip_self=True)
            P.op("dve", lambda h, Tp=Tp: h.tensor_tensor(tmpT[:, :], Tp[:, 0:64], Tf[:, :], ALU.add), r=[Tp, Tf], w=[tmpT])
            P.op("dve", lambda h, n=n: h.tensor_scalar(Tf[:, :], tmpT[:, :], PC[:, n:n + 1], None, ALU.mult), r=[tmpT, PC], w=[Tf])
            P.op("act", lambda h, n=n, Tn=Tn: h.activation(Tn[:, :], tmpT[:, :], AF.Identity, scale=PC[:, n:n + 1]), r=[tmpT, PC], w=[Tn])
            if hp == 0 and n < 2 and sample is None:
                dbg_dump(P, "zb%d" % n, zb[:, :], 64, [zb])
                dbg_dump(P, "ub%d" % n, ub[:, :], 64, [ub])
                dbg_dump(P, "Tf%d" % n, Tf[:, :], 64, [Tf])
                dbg_dump(P, "Tn%d" % n, Tn[:, :], 64, [Tn])
            if sample is not None and n < 4:
                pw2 = ps[n % 2]
                P.op("pe", lambda h, pw2=pw2: h.transpose(pw2[0:64, 0:128], Tf[:, :], ident_f[:, :]), r=[Tf, ident_f], w=[pw2], skip_self=True)
                P.op("act", lambda h, pw2=pw2: h.copy(wk[0:64, :], pw2[0:64, 0:128]), r=[pw2], w=[wk])
                for hh in range(2):
                    P.dma("pool", sample["wkv_dst"][n, 2 * hp + hh], wk[0:64, hh * 64:(hh + 1) * 64], r=[wk], w=[d_out])
            if c8 == 7:
                blk = n // 8
                P.op("act", lambda h, Yp=Yp, blk=blk: h.copy(yT[:, blk * 512:(blk + 1) * 512], Yp[:, :]), r=[Yp], w=[yT])
        if cfg.get("rw_phase", 9) <= 5:
            A.release()
            A.release()
            return
        if sample is None:
            pw_ = ps[4]
            P.op("pe", lambda h: h.transpose(pw_[0:64, 0:128], Tf[:, :], ident_f[:, :]), r=[Tf, ident_f], w=[pw_], skip_self=True)
            P.op("act", lambda h: h.copy(wk[0:64, :], pw_[0:64, 0:128]), r=[pw_], w=[wk])
            for hh in range(2):
                P.dma("pool", wkv_dst[2 * hp + hh], wk[0:64, hh * 64:(hh + 1) * 64], r=[wk], w=[d_out])
        else:
            P.op("dve", lambda h: h.memset(yT[:, 512:NT], 0.0), w=[yT])
        yb = A.alloc([512], BF16, "yb")
        t1 = A.alloc([512], F32, "t1")
        t2 = A.alloc([512], F32, "t2")
        for b in range(NT // 512):
            sl = slice(b * 512, (b + 1) * 512)
            pm, pv, pbn, pg = ps[0], ps[1], ps[2], ps[3]
            P.op("act", lambda h, sl=sl: h.copy(yb[:, :], yT[:, sl]), r=[yT], w=[yb])
            P.op("pe", lambda h, pm=pm: h.matmul(pm[:, :], lhsT=bd_b[:, :], rhs=yb[:, :], start=True, stop=True), r=[yb, bd_b], w=[pm], skip_self=True)
            P.op("dve", lambda h, pm=pm, sl=sl: h.scalar_tensor_tensor(yT[:, sl], pm[:, :], -1.0 / 64, yT[:, sl], ALU.mult, ALU.add), r=[pm, yT], w=[yT])
            P.op("act", lambda h, sl=sl: h.activation(yb[:, :], yT[:, sl], AF.Square), r=[yT], w=[yb])
            P.op("pe", lambda h, pv=pv: h.matmul(pv[:, :], lhsT=bd_b[:, :], rhs=yb[:, :], start=True, stop=True), r=[yb, bd_b], w=[pv], skip_self=True)
            P.op("act", lambda h, pv=pv: h.activation(t1[:, :], pv[:, :], AF.Ln, bias=R["gne"][:, :], scale=1.0 / 64), r=[pv, R["gne"]], w=[t1])
            P.op("act", lambda h: h.activation(t1[:, :], t1[:, :], AF.Exp, scale=-0.5), r=[t1], w=[t1])
            P.op("dve", lambda h, sl=sl: h.tensor_tensor(t1[:, :], t1[:, :], yT[:, sl], ALU.mult), r=[t1, yT], w=[t1])
            P.op("dve", lambda h: h.tensor_scalar(t1[:, :], t1[:, :], R["lnx_w"][:, hp:hp + 1], R["lnx_b"][:, hp:hp + 1], ALU.mult, ALU.add), r=[t1, R["lnx_w"], R["lnx_b"]], w=[t1])
            P.op("pe", lambda h, pbn=pbn, sl=sl: h.matmul(pbn[:, :], lhsT=bd_b[:, :], rhs=prod[:, sl], start=True, stop=True), r=[prod, bd_b], w=[pbn], skip_self=True)
            P.op("dve", lambda h, pbn=pbn, sl=sl: h.tensor_tensor(t2[:, :], pbn[:, :], V[:, sl], ALU.mult), r=[pbn, V], w=[t2])
            P.op("pool", lambda h: h.tensor_tensor(t1[:, :], t1[:, :], t2[:, :], ALU.add), r=[t1, t2], w=[t1])
            P.op("pe", lambda h, pg=pg, sl=sl: h.matmul(pg[:, :], lhsT=R["g2b"][:, hp * 128:(hp + 1) * 128], rhs=sg[:, sl], start=True, stop=True), r=[R["g2b"], sg], w=[pg], skip_self=True)
            P.op("dve", lambda h, pg=pg, sl=sl: h.tensor_tensor(orT[:, hp, sl], pg[:, :], t1[:, :], ALU.mult), r=[pg, t1], w=[orT])
        A.release()
        A.release()
    for hp_ in range(4):
        do_hp(hp_)
    A.release()


def add_proj(P, A, ps, xT_t, xT, srcT, nk, w_dram, NT=S):
    A.mark()
    wf = A.alloc([nk, 512], F32, "apw_f")
    wb = [A.alloc([nk, 512], BF16, "apw_b%d" % i) for i in range(2)]
    cnt = 0
    for half in range(2):
        P.dma("sp", wf[:, :, :], w_dram[:, half * 512:(half + 1) * 512].rearrange("(k p) c -> p k c", p=128), w=[wf])
        P.op("pool", lambda h, half=half: h.tensor_copy(wb[half][:, :, :], wf[:, :, :]), r=[wf], w=[wb[half]])
        for mm in range(4):
            m = half * 4 + mm
            for b in range(NT // 512):
                pm = ps[cnt % 4]
                cnt += 1
                for k in range(nk):
                    P.op("pe", lambda h, pm=pm, half=half, mm=mm, k=k, b=b: h.matmul(pm[:, :], lhsT=wb[half][:, k, mm * 128:(mm + 1) * 128], rhs=srcT[:, k, b * 512:(b + 1) * 512], start=(k == 0), stop=(k == nk - 1)),
                         r=[wb[half], srcT], w=[pm], skip_self=True)
                dst = xT_t[:, m, b * 512:(b + 1) * 512]
                P.op("dve", lambda h, dst=dst, pm=pm: h.tensor_tensor(dst, pm[:, :], dst, ALU.add), r=[pm, xT[b]], w=[xT[b]])
    A.release()


def ffn(P, A, ps, xT_t, xT, g, gl, ones_b, epsc, wg_d, wu_d, wd_d):
    NF = DFF // 128
    for half in range(2):
        A.mark()
        aT = A.alloc([NF, 1024], BF16, "aT")
        A.mark()
        hT = [A.alloc([8, 512], BF16, "fh%d" % b) for b in range(2)]
        rmsnorm_T(P, A, ps, xT_t, xT, hT, g, gl, ones_b, epsc, blocks=[half * 2, half * 2 + 1])
        wsg = [A.alloc([8, 256], F32, "wsg%d" % i) for i in range(2)]
        wsu = [A.alloc([8, 256], F32, "wsu%d" % i) for i in range(2)]
        wbg = [A.alloc([8, 256], BF16, "wbg%d" % i) for i in range(2)]
        wbu = [A.alloc([8, 256], BF16, "wbu%d" % i) for i in range(2)]
        sil = [A.alloc([512], F32, "sil%d" % i) for i in range(2)]
        cnt = 0
        for pc in range(NF // 2):
            i = pc % 2
            P.dma("sp", wsg[i][:, :, :], wg_d[:, pc * 256:(pc + 1) * 256].rearrange("(k p) c -> p k c", p=128), w=[wsg[i]])
            P.dma("sp", wsu[i][:, :, :], wu_d[:, pc * 256:(pc + 1) * 256].rearrange("(k p) c -> p k c", p=128), w=[wsu[i]])
            P.op("pool", lambda h, i=i: h.tensor_copy(wbg[i][:, :, :], wsg[i][:, :, :]), r=[wsg[i]], w=[wbg[i]])
            P.op("dve", lambda h, i=i: h.tensor_copy(wbu[i][:, :, :], wsu[i][:, :, :]), r=[wsu[i]], w=[wbu[i]])
            for cc in range(2):
                f = pc * 2 + cc
                for b in range(2):
                    pg, pu = ps[cnt % 2], ps[2 + cnt % 2]
                    sl_ = sil[cnt % 2]
                    cnt += 1
                    for k in range(8):
                        P.op("pe", lambda h, pg=pg, i=i, k=k, cc=cc, b=b: h.matmul(pg[:, :], lhsT=wbg[i][:, k, cc * 128:(cc + 1) * 128], rhs=hT[b][:, k, :], start=(k == 0), stop=(k == 7)), r=[wbg[i], hT[b]], w=[pg], skip_self=True)
                    for k in range(8):
                        P.op("pe", lambda h, pu=pu, i=i, k=k, cc=cc, b=b: h.matmul(pu[:, :], lhsT=wbu[i][:, k, cc * 128:(cc + 1) * 128], rhs=hT[b][:, k, :], start=(k == 0), stop=(k == 7)), r=[wbu[i], hT[b]], w=[pu], skip_self=True)
                    P.op("act", lambda h, sl_=sl_, pg=pg: h.activation(sl_[:, :], pg[:, :], AF.Silu), r=[pg], w=[sl_])
                    P.op("dve", lambda h, sl_=sl_, pu=pu, f=f, b=b: h.tensor_tensor(aT[:, f, b * 512:(b + 1) * 512], pu[:, :], sl_[:, :], ALU.mult), r=[pu, sl_], w=[aT])
        A.release()
        A.mark()
        wdf = [A.alloc([1024], F32, "wdf%d" % i) for i in range(2)]
        wdb = A.alloc([NF, 1024], BF16, "wdb")
        for f in range(NF):
            P.dma("sp", wdf[f % 2][:, :], wd_d[f * 128:(f + 1) * 128, :], w=[wdf[f % 2]])
            eng = "pool" if f % 2 == 0 else "act"
            if eng == "pool":
                P.op("pool", lambda h, f=f: h.tensor_copy(wdb[:, f, :], wdf[f % 2][:, :]), r=[wdf[f % 2]], w=[wdb])
            else:
                P.op("act", lambda h, f=f: h.copy(wdb[:, f, :], wdf[f % 2][:, :]), r=[wdf[f % 2]], w=[wdb])
        cnt = 0
        for b in range(2):
            bb = half * 2 + b
            for m in range(8):
                pm = ps[4 + cnt % 4]
                cnt += 1
                for f in range(NF):
                    P.op("pe", lambda h, pm=pm, f=f, m=m, b=b: h.matmul(pm[:, :], lhsT=wdb[:, f, m * 128:(m + 1) * 128], rhs=aT[:, f, b * 512:(b + 1) * 512], start=(f == 0), stop=(f == NF - 1)), r=[wdb, aT], w=[pm], skip_self=True)
                dst = xT_t[:, m, bb * 512:(bb + 1) * 512]
                P.op("dve", lambda h, dst=dst, pm=pm: h.tensor_tensor(dst, pm[:, :], dst, ALU.add), r=[pm, xT[bb]], w=[xT[bb]])
        A.release()
        A.release()


def conv_mix(P, A, ps, xT_t, xT, g, gl, ones_b, epsc, w_in_c, conv_w_col, w_out_c, conv_dst, d_out):
    A.mark()
    zT = A.alloc([8, S], BF16, "zT")
    A.mark()
    hT = [A.alloc([8, 512], BF16, "ch%d" % b) for b in range(4)]
    rmsnorm_T(P, A, ps, xT_t, xT, hT, g, gl, ones_b, epsc)
    ws = [A.alloc([3, 8, 128], F32, "cws%d" % i) for i in range(2)]
    wb = [A.alloc([3, 8, 128], BF16, "cwb%d" % i) for i in range(2)]
    u = A.alloc([S + 2], F32, "cu")
    bg = A.alloc([S], F32, "cbg")
    y = A.alloc([S], F32, "cy")
    cnt = 0
    for m in range(8):
        i = m % 2
        for j in range(3):
            c0 = j * 1024 + m * 128
            P.dma("sp", ws[i][:, j, :, :], w_in_c[:, c0:c0 + 128].rearrange("(k p) c -> p k c", p=128), w=[ws[i]])
        P.op("pool", lambda h, i=i: h.tensor_copy(wb[i][:, :, :, :], ws[i][:, :, :, :]), r=[ws[i]], w=[wb[i]])
        P.op("pool", lambda h: h.memset(u[:, 0:2], 0.0), w=[u])
        for b in range(4):
            pp = [ps[(cnt * 3 + j) % 6] for j in range(3)]
            cnt += 1
            for j in range(3):
                for k in range(8):
                    P.op("pe", lambda h, j=j, k=k, b=b, i=i, pj=pp[j]: h.matmul(pj[:, :], lhsT=wb[i][:, j, k, :], rhs=hT[b][:, k, :], start=(k == 0), stop=(k == 7)), r=[wb[i], hT[b]], w=[pp[j]], skip_self=True)
            sl = slice(b * 512, (b + 1) * 512)
            P.op("act", lambda h, sl=sl, p0=pp[0]: h.copy(bg[:, sl], p0[:, :]), r=[pp[0]], w=[bg])
            P.op("act", lambda h, b=b, p1=pp[1]: h.copy(y[:, b * 512:(b + 1) * 512], p1[:, :]), r=[pp[1]], w=[y])
            P.op("dve", lambda h, b=b, p2=pp[2]: h.tensor_tensor(u[:, 2 + b * 512:2 + (b + 1) * 512], p2[:, :], y[:, b * 512:(b + 1) * 512], ALU.mult), r=[pp[2], y], w=[u])
        P.dma("pool", conv_dst[:, m * 128:(m + 1) * 128].rearrange("r p -> p r"), u[:, S:S + 2], r=[u], w=[d_out], allow_slow_non_contiguous=True)
        P.op("dve", lambda h, m=m: h.tensor_scalar(y[:, :], u[:, 0:S], conv_w_col[:, 0, m:m + 1], None, ALU.mult), r=[u, conv_w_col], w=[y])
        P.op("dve", lambda h, m=m: h.scalar_tensor_tensor(y[:, :], u[:, 1:S + 1], conv_w_col[:, 1, m:m + 1], y[:, :], ALU.mult, ALU.add), r=[u, y, conv_w_col], w=[y])
        P.op("dve", lambda h, m=m: h.scalar_tensor_tensor(y[:, :], u[:, 2:S + 2], conv_w_col[:, 2, m:m + 1], y[:, :], ALU.mult, ALU.add), r=[u, y, conv_w_col], w=[y])
        P.op("pool", lambda h, m=m: h.tensor_tensor(zT[:, m, :], bg[:, :], y[:, :], ALU.mult), r=[bg, y], w=[zT])
    A.release()
    add_proj(P, A, ps, xT_t, xT, zT, 8, w_out_c)
    A.release()


def store_xT(P, A, ps, xT_t, xT, ident_f, y_dst, d_out):
    A.mark()
    st = [A.alloc([1024], F32, "yst%d" % i) for i in range(2)]
    for tb in range(16):
        s_ = st[tb % 2]
        for half in range(2):
            pb = ps[(tb * 2 + half) % 4]
            for kk in range(4):
                k = half * 4 + kk
                P.op("pe", lambda h, pb=pb, kk=kk, k=k, tb=tb: h.transpose(pb[:, kk * 128:(kk + 1) * 128], xT_t[:, k, tb * 128:(tb + 1) * 128], ident_f[:, :]), r=[xT[tb // 4], ident_f], w=[pb], skip_self=True)
            if half == 0:
                P.op("act", lambda h, pb=pb, s_=s_: h.copy(s_[:, 0:512], pb[:, :]), r=[pb], w=[s_])
            else:
                P.op("dve", lambda h, pb=pb, s_=s_: h.tensor_copy(s_[:, 512:1024], pb[:, :]), r=[pb], w=[s_])
        P.dma("pool", y_dst[tb * 128:(tb + 1) * 128, :], s_[:, :], r=[s_], w=[d_out])
    A.release()


def rmsnorm_T(P, A, ps, xT_t, xT, hT, g, gl, ones_b, epsc, blocks=None, BW=512):
    A.mark()
    sq = [A.alloc([8, BW], BF16, "nsq%d" % i) for i in range(2)]
    rs = [A.alloc([BW], F32, "nrs%d" % i) for i in range(2)]
    if blocks is None:
        blocks = [0, 1, 2, 3]
    for li, b in enumerate(blocks):
        sqt, r_ = sq[b % 2], rs[b % 2]
        pm = ps[4 + b % 2]
        src = xT_t[:, :, b * BW:(b + 1) * BW]
        P.op("act", lambda h, sqt=sqt, src=src: h.activation(sqt[:, :, :], src, AF.Square), r=[xT[b]], w=[sqt])
        for k in range(8):
            P.op("pe", lambda h, pm=pm, sqt=sqt, k=k: h.matmul(pm[:, 0:BW], lhsT=ones_b[:, :], rhs=sqt[:, k, :], start=(k == 0), stop=(k == 7)), r=[sqt, ones_b], w=[pm], skip_self=True)
        P.op("act", lambda h, r_=r_, pm=pm: h.activation(r_[:, :], pm[:, 0:BW], AF.Ln, bias=epsc[:, :], scale=1.0 / 1024), r=[pm, epsc], w=[r_])
        P.op("act", lambda h, r_=r_: h.activation(r_[:, :], r_[:, :], AF.Exp, scale=-0.5), r=[r_], w=[r_])
        for k in range(8):
            P.op("dve", lambda h, b=b, k=k, r_=r_, li=li: h.scalar_tensor_tensor(hT[li][:, k, :], xT_t[:, k, b * BW:(b + 1) * BW], g[:, gl, k:k + 1], r_[:, :], ALU.mult, ALU.mult),
                 r=[xT[b], r_, g], w=[hT[li]])
    A.release()


def add_proj(P, A, ps, xT_t, xT, srcT, nk, w_dram, NT=S, BW=512):
    A.mark()
    wf = A.alloc([nk, 512], F32, "apw_f")
    wb = [A.alloc([nk, 512], BF16, "apw_b%d" % i) for i in range(2)]
    cnt = 0
    for half in range(2):
        P.dma("sp", wf[:, :, :], w_dram[:, half * 512:(half + 1) * 512].rearrange("(k p) c -> p k c", p=128), w=[wf])
        P.op("pool", lambda h, half=half: h.tensor_copy(wb[half][:, :, :], wf[:, :, :]), r=[wf], w=[wb[half]])
        for mm in range(4):
            m = half * 4 + mm
            for b in range(NT // BW):
                pm = ps[cnt % 4]
                cnt += 1
                for k in range(nk):
                    P.op("pe", lambda h, pm=pm, half=half, mm=mm, k=k, b=b: h.matmul(pm[:, 0:BW], lhsT=wb[half][:, k, mm * 128:(mm + 1) * 128], rhs=srcT[:, k, b * BW:(b + 1) * BW], start=(k == 0), stop=(k == nk - 1)),
                         r=[wb[half], srcT], w=[pm], skip_self=True)
                dst = xT_t[:, m, b * BW:(b + 1) * BW]
                P.op("dve", lambda h, dst=dst, pm=pm: h.tensor_tensor(dst, pm[:, 0:BW], dst, ALU.add), r=[pm, xT[b]], w=[xT[b]])
    A.release()


def ffn(P, A, ps, xT_t, xT, g, gl, ones_b, epsc, wg_d, wu_d, wd_d, BW=512, NHALF=2, NBH=2):
    NF = DFF // 128
    for half in range(NHALF):
        A.mark()
        aT = A.alloc([NF, NBH * BW], BF16, "aT")
        A.mark()
        hT = [A.alloc([8, BW], BF16, "fh%d" % b) for b in range(NBH)]
        rmsnorm_T(P, A, ps, xT_t, xT, hT, g, gl, ones_b, epsc, blocks=[half * NBH + b for b in range(NBH)], BW=BW)
        wsg = [A.alloc([8, 256], F32, "wsg%d" % i) for i in range(2)]
        wsu = [A.alloc([8, 256], F32, "wsu%d" % i) for i in range(2)]
        wbg = [A.alloc([8, 256], BF16, "wbg%d" % i) for i in range(2)]
        wbu = [A.alloc([8, 256], BF16, "wbu%d" % i) for i in range(2)]
        sil = [A.alloc([BW], F32, "sil%d" % i) for i in range(2)]
        cnt = 0
        for pc in range(NF // 2):
            i = pc % 2
            P.dma("sp", wsg[i][:, :, :], wg_d[:, pc * 256:(pc + 1) * 256].rearrange("(k p) c -> p k c", p=128), w=[wsg[i]])
            P.dma("sp", wsu[i][:, :, :], wu_d[:, pc * 256:(pc + 1) * 256].rearrange("(k p) c -> p k c", p=128), w=[wsu[i]])
            P.op("pool", lambda h, i=i: h.tensor_copy(wbg[i][:, :, :], wsg[i][:, :, :]), r=[wsg[i]], w=[wbg[i]])
            P.op("dve", lambda h, i=i: h.tensor_copy(wbu[i][:, :, :], wsu[i][:, :, :]), r=[wsu[i]], w=[wbu[i]])
            for cc in range(2):
                f = pc * 2 + cc
                for b in range(NBH):
                    pg, pu = ps[cnt % 2], ps[2 + cnt % 2]
                    sl_ = sil[cnt % 2]
                    cnt += 1
                    for k in range(8):
                        P.op("pe", lambda h, pg=pg, i=i, k=k, cc=cc, b=b: h.matmul(pg[:, 0:BW], lhsT=wbg[i][:, k, cc * 128:(cc + 1) * 128], rhs=hT[b][:, k, :], start=(k == 0), stop=(k == 7)), r=[wbg[i], hT[b]], w=[pg], skip_self=True)
                    for k in range(8):
                        P.op("pe", lambda h, pu=pu, i=i, k=k, cc=cc, b=b: h.matmul(pu[:, 0:BW], lhsT=wbu[i][:, k, cc * 128:(cc + 1) * 128], rhs=hT[b][:, k, :], start=(k == 0), stop=(k == 7)), r=[wbu[i], hT[b]], w=[pu], skip_self=True)
                    P.op("act", lambda h, sl_=sl_, pg=pg: h.activation(sl_[:, :], pg[:, 0:BW], AF.Silu), r=[pg], w=[sl_])
                    P.op("dve", lambda h, sl_=sl_, pu=pu, f=f, b=b: h.tensor_tensor(aT[:, f, b * BW:(b + 1) * BW], pu[:, 0:BW], sl_[:, :], ALU.mult), r=[pu, sl_], w=[aT])
        A.release()
        A.mark()
        wdf = [A.alloc([1024], F32, "wdf%d" % i) for i in range(2)]
        wdb = A.alloc([NF, 1024], BF16, "wdb")
        for f in range(NF):
            P.dma("sp", wdf[f % 2][:, :], wd_d[f * 128:(f + 1) * 128, :], w=[wdf[f % 2]])
            if f % 2 == 0:
                P.op("pool", lambda h, f=f: h.tensor_copy(wdb[:, f, :], wdf[f % 2][:, :]), r=[wdf[f % 2]], w=[wdb])
            else:
                P.op("act", lambda h, f=f: h.copy(wdb[:, f, :], wdf[f % 2][:, :]), r=[wdf[f % 2]], w=[wdb])
        cnt = 0
        for b in range(NBH):
            bb = half * NBH + b
            for m in range(8):
                pm = ps[4 + cnt % 4]
                cnt += 1
                for f in range(NF):
                    P.op("pe", lambda h, pm=pm, f=f, m=m, b=b: h.matmul(pm[:, 0:BW], lhsT=wdb[:, f, m * 128:(m + 1) * 128], rhs=aT[:, f, b * BW:(b + 1) * BW], start=(f == 0), stop=(f == NF - 1)), r=[wdb, aT], w=[pm], skip_self=True)
                dst = xT_t[:, m, bb * BW:(bb + 1) * BW]
                P.op("dve", lambda h, dst=dst, pm=pm: h.tensor_tensor(dst, pm[:, 0:BW], dst, ALU.add), r=[pm, xT[bb]], w=[xT[bb]])
        A.release()
        A.release()


NS = 32
SPAD = 1024


def head_norm(P, pm, N, sq, lv, p2, bd_b, epsc):
    P.op("act", lambda h: h.activation(sq[:, 0:N], pm[:, 0:N], AF.Square), r=[pm], w=[sq])
    P.op("pe", lambda h: h.matmul(p2[:, 0:N], lhsT=bd_b[:, :], rhs=sq[:, 0:N], start=True, stop=True), r=[sq, bd_b], w=[p2], skip_self=True)
    P.op("act", lambda h: h.activation(lv[:, 0:N], p2[:, 0:N], AF.Ln, bias=epsc[:, :], scale=1.0 / 64), r=[p2, epsc], w=[lv])
    P.op("act", lambda h: h.activation(lv[:, 0:N], lv[:, 0:N], AF.Exp, scale=-0.5), r=[lv], w=[lv])


def sample_path(P, A, cst, ps, psb, C, Dm, cfg):
    ident_f, ident_b, ones_b, bd_b, epsc = C["ident_f"], C["ident_b"], C["ones_b"], C["bd_b"], C["epsc"]
    d_out = Dm["d_out"]
    A.mark()
    xs_t = A.alloc([8, NS], F32, "xsT")
    xsT = [xs_t]
    xs_ap = xs_t[:, :, :]
    st = A.alloc([D], F32, "xs_st")
    P.dma("sp", st[0:NS, :], Dm["xs_own"], w=[st])
    pb = ps[0]
    for k in range(8):
        P.op("pe", lambda h, k=k: h.transpose(pb[:, k * NS:(k + 1) * NS], st[0:NS, k * 128:(k + 1) * 128], ident_f[0:NS, 0:NS]), r=[st, ident_f], w=[pb], skip_self=True)
    P.op("act", lambda h: h.copy(xs_ap, pb[:, 0:8 * NS].rearrange("p (a b) -> p a b", a=8)), r=[pb], w=[xs_t])
    if cfg.get("s_stop", 9) <= 0:
        return
    A.mark()
    hs = A.alloc([8, NS], BF16, "hsT")
    rmsnorm_T(P, A, ps, xs_ap, xsT, [hs], C["g_mix"], 0, ones_b, epsc, blocks=[0], BW=NS)
    qs = A.alloc([4, NS], BF16, "qsT")
    ks = A.alloc([4, NS], BF16, "ksT")
    vtok = A.alloc([512], BF16, "vs_tok")
    ktokf = A.alloc([512], F32, "ks_tokf")
    vtokf = A.alloc([512], F32, "vs_tokf")
    pad = A.alloc([SPAD], F32, "prw_pad")
    shf = A.alloc([14, 4], F32, "shf")
    kf = A.alloc([4, 128], F32, "ks_f")
    vf = A.alloc([4, 128], F32, "vs_f")
    P.op("dve", lambda h: h.memset(kf[:, :, :], 0.0), w=[kf])
    P.op("dve", lambda h: h.memset(vf[:, :, :], 0.0), w=[vf])
    P.op("dve", lambda h: h.memset(pad[:, :], 0.0), w=[pad])
    A.mark()
    wst = [A.alloc([8, 256], F32, "swst%d" % i) for i in range(2)]
    wbf = [A.alloc([8, 256], BF16, "swbf%d" % i) for i in range(2)]
    sq = A.alloc([NS], BF16, "ssq")
    lv = A.alloc([NS], F32, "slv")
    ev = [A.alloc([128], F32, "sev%d" % i) for i in range(2)]
    for e_ in ev:
        P.op("dve", lambda h, e_=e_: h.memset(e_[:, :], 0.0), w=[e_])
    for pc in range(13):
        ws, wb = wst[pc % 2], wbf[pc % 2]
        P.dma("sp", ws[:, :, :], Dm["w_in_ab"][:, pc * 256:(pc + 1) * 256].rearrange("(k p) c -> p k c", p=128), w=[ws])
        P.op("pool" if pc % 2 == 0 else "dve", lambda h, ws=ws, wb=wb: h.tensor_copy(wb[:, :, :], ws[:, :, :]), r=[ws], w=[wb])
        for cc in range(2):
            c = pc * 2 + cc
            pm = ps[c % 4]
            e = ev[c % 2]
            for k in range(8):
                P.op("pe", lambda h, pm=pm, wb=wb, k=k, cc=cc: h.matmul(pm[:, 0:NS], lhsT=wb[:, k, cc * 128:(cc + 1) * 128], rhs=hs[:, k, :], start=(k == 0), stop=(k == 7)), r=[wb, hs], w=[pm], skip_self=True)
            if c < 8:
                head_norm(P, pm, NS, sq, lv, ps[4 + c % 2], bd_b, epsc)
                if c < 4:
                    P.op("dve", lambda h, pm=pm, c=c: h.scalar_tensor_tensor(qs[:, c, :], pm[:, 0:NS], C["gq8"][:, :], lv[:, :], ALU.mult, ALU.mult), r=[pm, lv, C["gq8"]], w=[qs])
                    continue
                P.op("dve", lambda h, pm=pm, e=e: h.scalar_tensor_tensor(e[:, 0:NS], pm[:, 0:NS], C["gk"][:, :], lv[:, :], ALU.mult, ALU.mult), r=[pm, lv, C["gk"]], w=[e])
                P.op("pool", lambda h, e=e, c=c: h.tensor_copy(ks[:, c - 4, :], e[:, 0:NS]), r=[e], w=[ks])
            else:
                P.op("act", lambda h, e=e, pm=pm: h.copy(e[:, 0:NS], pm[:, 0:NS]), r=[pm], w=[e])
            if c >= 12 and cfg.get("sk", 0) & 1:
                continue
            if c >= 12:
                ci = c - 12
                P.op("dve", lambda h, e=e: h.tensor_copy(pad[:, :].rearrange("p (n t) -> p n t", t=64)[:, 0:4, 0:8], e[:, 0:NS].rearrange("p (b t) -> p b t", t=8)), r=[e], w=[pad])
                P.dma("pool", Dm["prw_s"][ci], pad[:, :], r=[pad], w=[Dm["d_prws"]])
                P.op("pool", lambda h, e=e, ci=ci: h.tensor_copy(shf[:, ci, :], e[:, 0:NS].rearrange("p (b t) -> p b t", t=8)[:, :, 7]), r=[e], w=[shf])
                continue
            kvf = kf if c < 8 else vf
            P.op("pool", lambda h, e=e, kvf=kvf, c=c: h.tensor_copy(kvf[:, (c - 4) % 4, 0:NS], e[:, 0:NS]), r=[e], w=[kvf])
    d_vs = T(None, "d_vs")
    for hp in range(4):
        P.dma("pool", Dm["k_s"][:, hp * 128:(hp + 1) * 128].rearrange("t p -> p t"), kf[:, hp, 0:NS], r=[kf], w=[d_out], allow_slow_non_contiguous=True)
        P.dma("pool", Dm["v_s"][:, hp * 128:(hp + 1) * 128].rearrange("t p -> p t"), vf[:, hp, 0:NS], r=[vf], w=[d_vs], allow_slow_non_contiguous=True)
    P.dma("sp", vtokf[0:NS, :], Dm["v_s"], r=[d_vs], w=[vtokf])
    P.op("dve", lambda h: h.tensor_copy(vtok[0:NS, :], vtokf[0:NS, :]), r=[vtokf], w=[vtok])
    Dm["d_vs"] = d_vs
    A.release()
    if cfg.get("sk", 0) & 4:
        return
    for ci in range(14):
        P.dma("pool", Dm["shift_s"][:, ci * 128:(ci + 1) * 128].rearrange("b p -> p b"), shf[:, ci, :], r=[shf], w=[d_out], allow_slow_non_contiguous=True)
    if cfg.get("s_stop", 9) <= 1:
        return
    osT = A.alloc([4, NS], BF16, "osT")
    if cfg.get("s_attn", True):
        sample_attn(P, A, cst, ps, psb, C, Dm, qs, ks, vtok, osT, cfg)
    else:
        P.op("dve", lambda h: h.memset(osT[:, :, :], 0.0), w=[osT])
    add_proj(P, A, ps, xs_ap, xsT, osT, 4, Dm["w_out_ab"][0:512, :], NT=NS, BW=NS)
    if cfg.get("s_stop", 9) <= 2:
        return
    orP = A.alloc([4, SPAD], BF16, "orP")
    rwkv_seq(P, A, ps, psb, C["R"], Dm["prw_s"], Dm["d_prws"], orP, bd_b, ident_f, ident_b, None, d_out, cfg, NT=SPAD,
             sample=dict(state_wkv=Dm["state_wkv"], state_shift=Dm["state_shift"], wkv_dst=Dm["wkv_s"], padmask=C["padmask"]))
    orT = A.alloc([4, NS], BF16, "orTs")
    P.op("dve", lambda h: h.tensor_copy(orT[:, :, :].rearrange("p k (b t) -> p k b t", t=8), orP[:, :, :].rearrange("p k (n t) -> p k n t", t=64)[:, :, 0:4, 0:8]), r=[orP], w=[orT])
    add_proj(P, A, ps, xs_ap, xsT, orT, 4, Dm["w_out_ab"][512:1024, :], NT=NS, BW=NS)
    A.release()
    if cfg.get("s_stop", 9) <= 3:
        return
    ffn(P, A, ps, xs_ap, xsT, C["g_ffn"], 0, ones_b, epsc, Dm["w_gate"][0], Dm["w_up"][0], Dm["w_down"][0], BW=NS, NHALF=1, NBH=1)
    if cfg.get("s_stop", 9) <= 4:
        return
    conv_small(P, A, ps, xs_ap, xsT, C, Dm)
    if cfg.get("s_stop", 9) <= 5:
        return
    ffn(P, A, ps, xs_ap, xsT, C["g_ffn"], 1, ones_b, epsc, Dm["w_gate"][1], Dm["w_up"][1], Dm["w_down"][1], BW=NS, NHALF=1, NBH=1)
    for k in range(8):
        P.dma("pool", Dm["y_s"][:, k * 128:(k + 1) * 128].rearrange("t p -> p t"), xs_t[:, k, :], r=[xs_t], w=[d_out], allow_slow_non_contiguous=True)
    A.release()


def conv_small(P, A, ps, xs_ap, xsT, C, Dm):
    ones_b, epsc = C["ones_b"], C["epsc"]
    cwc = C["cwc"]
    A.mark()
    zT = A.alloc([8, NS], BF16, "czT")
    A.mark()
    hs = A.alloc([8, NS], BF16, "chs")
    rmsnorm_T(P, A, ps, xs_ap, xsT, [hs], C["g_mix"], 1, ones_b, epsc, blocks=[0], BW=NS)
    ws = [A.alloc([3, 8, 128], F32, "scws%d" % i) for i in range(2)]
    wb = [A.alloc([3, 8, 128], BF16, "scwb%d" % i) for i in range(2)]
    u = [A.alloc([4, 10], F32, "scu%d" % i) for i in range(2)]
    bg = A.alloc([NS], F32, "scbg")
    cg = A.alloc([NS], F32, "sccg")
    y = A.alloc([NS], F32, "scy")
    for m in range(8):
        i = m % 2
        um = u[i]
        for j in range(3):
            c0 = j * 1024 + m * 128
            P.dma("sp", ws[i][:, j, :, :], Dm["w_in_c"][:, c0:c0 + 128].rearrange("(k p) c -> p k c", p=128), w=[ws[i]])
        P.op("pool", lambda h, i=i: h.tensor_copy(wb[i][:, :, :, :], ws[i][:, :, :, :]), r=[ws[i]], w=[wb[i]])
        pp = [ps[(m * 3 + j) % 6] for j in range(3)]
        for j in range(3):
            for k in range(8):
                P.op("pe", lambda h, j=j, k=k, i=i, pj=pp[j]: h.matmul(pj[:, 0:NS], lhsT=wb[i][:, j, k, :], rhs=hs[:, k, :], start=(k == 0), stop=(k == 7)), r=[wb[i], hs], w=[pp[j]], skip_self=True)
        P.op("act", lambda h, p0=pp[0]: h.copy(bg[:, :], p0[:, 0:NS]), r=[pp[0]], w=[bg])
        P.op("act", lambda h, p1=pp[1]: h.copy(cg[:, :], p1[:, 0:NS]), r=[pp[1]], w=[cg])
        for b in range(4):
            P.dma("sp", um[:, b, 0:2], Dm["state_conv"][b, :, m * 128:(m + 1) * 128].rearrange("r p -> p r"), w=[um], allow_slow_non_contiguous=True)
        P.op("dve", lambda h, um=um, p2=pp[2]: h.tensor_tensor(um[:, :, 2:10], p2[:, 0:NS].rearrange("p (b t) -> p b t", t=8), cg[:, :].rearrange("p (b t) -> p b t", t=8), ALU.mult), r=[pp[2], cg], w=[um])
        for b in range(4):
            P.dma("pool", Dm["conv_s"][b, :, m * 128:(m + 1) * 128].rearrange("r p -> p r"), um[:, b, 8:10], r=[um], w=[Dm["d_out"]], allow_slow_non_contiguous=True)
        y3 = y[:, :].rearrange("p (b t) -> p b t", t=8)
        P.op("dve", lambda h, um=um, m=m, y3=y3: h.tensor_scalar(y3, um[:, :, 0:8], cwc[:, 0, m:m + 1], None, ALU.mult), r=[um, cwc], w=[y])
        P.op("dve", lambda h, um=um, m=m, y3=y3: h.scalar_tensor_tensor(y3, um[:, :, 1:9], cwc[:, 1, m:m + 1], y3, ALU.mult, ALU.add), r=[um, y, cwc], w=[y])
        P.op("dve", lambda h, um=um, m=m, y3=y3: h.scalar_tensor_tensor(y3, um[:, :, 2:10], cwc[:, 2, m:m + 1], y3, ALU.mult, ALU.add), r=[um, y, cwc], w=[y])
        P.op("pool", lambda h, m=m: h.tensor_tensor(zT[:, m, :], bg[:, :], y[:, :], ALU.mult), r=[bg, y], w=[zT])
    A.release()
    add_proj(P, A, ps, xs_ap, xsT, zT, 8, Dm["w_out_c"], NT=NS, BW=NS)
    A.release()


NPG = 640
TABW = 4160


def sample_attn(P, A, cst, ps, psb, C, Dm, qs, ks, vtok, osT, cfg):
    ident_f, ident_b, ones_b, bd_b, epsc = C["ident_f"], C["ident_b"], C["ones_b"], C["bd_b"], C["epsc"]
    negU, io_f, bias_bc = C["negU"], C["io_f"], C["bias_bc"]
    d_out = Dm["d_out"]
    ACTDVE = ["act", "dve"]

    def cp(eng, dst, src, r, w):
        if eng == "act":
            P.op("act", lambda h: h.copy(dst, src), r=r, w=w)
        else:
            P.op(eng, lambda h: h.tensor_copy(dst, src), r=r, w=w)

    A.mark()
    ones_f = A.alloc([128], F32, "ones_f")
    P.op("dve", lambda h: h.memset(ones_f[:, :], 1.0), w=[ones_f])
    sufU = A.alloc([128], F32, "sufU")
    P.op("dve", lambda h: h.tensor_single_scalar(sufU[:, :], io_f[:, :], 0.0, ALU.is_lt), r=[io_f], w=[sufU])
    QT = A.alloc([4, 256], BF16, "QTall")
    QT2m = A.alloc([64, 128], BF16, "QT2m")
    OHT = A.alloc([NPG], BF16, "OHT")
    A.mark()
    xa_t = A.alloc([8, 256], F32, "xaT")
    A.mark()
    st2 = [A.alloc([D], F32, "xa_st%d" % i) for i in range(2)]
    for tb in range(2):
        st = st2[tb]
        P.dma("sp", st[:, :], Dm["xs_all"][tb * 128:(tb + 1) * 128, :], w=[st])
        for half in range(2):
            pb = ps[half]
            for kk in range(4):
                k = half * 4 + kk
                P.op("pe", lambda h, pb=pb, kk=kk, k=k, st=st: h.transpose(pb[:, kk * 128:(kk + 1) * 128], st[:, k * 128:(k + 1) * 128], ident_f[:, :]), r=[st, ident_f], w=[pb], skip_self=True)
            cp(ACTDVE[half], xa_t[:, half * 4:half * 4 + 4, tb * 128:(tb + 1) * 128], pb[:, :].rearrange("p (a b) -> p a b", a=4), [pb], [xa_t])
    A.release()
    ha = A.alloc([8, 256], BF16, "haT")
    rmsnorm_T(P, A, ps, xa_t[:, :, :], [xa_t], [ha], C["g_mix"], 0, ones_b, epsc, blocks=[0], BW=256)
    A.mark()
    wst = [A.alloc([8, 256], F32, "qwst%d" % i) for i in range(2)]
    wbf = [A.alloc([8, 256], BF16, "qwbf%d" % i) for i in range(2)]
    sq = A.alloc([256], BF16, "qsq")
    lv = A.alloc([256], F32, "qlv")
    for pc in range(2):
        ws, wb = wst[pc], wbf[pc]
        P.dma("sp", ws[:, :, :], Dm["w_in_ab"][:, pc * 256:(pc + 1) * 256].rearrange("(k p) c -> p k c", p=128), w=[ws])
        P.op("pool", lambda h, ws=ws, wb=wb: h.tensor_copy(wb[:, :, :], ws[:, :, :]), r=[ws], w=[wb])
        for cc in range(2):
            c = pc * 2 + cc
            pm = ps[c % 4]
            for k in range(8):
                P.op("pe", lambda h, pm=pm, wb=wb, k=k, cc=cc: h.matmul(pm[:, 0:256], lhsT=wb[:, k, cc * 128:(cc + 1) * 128], rhs=ha[:, k, :], start=(k == 0), stop=(k == 7)), r=[wb, ha], w=[pm], skip_self=True)
            head_norm(P, pm, 256, sq, lv, ps[4 + c % 2], bd_b, epsc)
            P.op("dve", lambda h, pm=pm, c=c: h.scalar_tensor_tensor(QT[:, c, :], pm[:, 0:256], C["gq8"][:, :], lv[:, :], ALU.mult, ALU.mult), r=[pm, lv, C["gq8"]], w=[QT])
    A.release()
    A.release()
    A.mark()
    qtk = A.alloc([2, 512], BF16, "qtk")
    for half in range(2):
        pb = 6 + half
        for hp in range(4):
            P.op("pe", lambda h, pb=pb, hp=hp, half=half: h.transpose(psb[pb][:, hp * 128:(hp + 1) * 128], QT[:, hp, half * 128:(half + 1) * 128], ident_b[:, :]), r=[QT, ident_b], w=[ps[pb]], skip_self=True)
        cp(ACTDVE[half], qtk[:, half, :], psb[pb][:, 0:512], [ps[pb]], [qtk])
    d_q = T(None, "d_q")
    P.dma("pool", Dm["qtok_d"].rearrange("(h p) c -> p h c", p=128), qtk[:, :, :], r=[qtk], w=[d_q])
    Qb = A.alloc([8, 512], BF16, "Qb")
    P.dma("sp", Qb[0:32, :, :], Dm["qtok_d"].rearrange("(b t) c -> b t c", t=8), r=[d_q], w=[Qb])
    P.op("dve", lambda h: h.memset(QT2m[0:32, :, :], 0.0), w=[QT2m])
    for h2 in range(2):
        dst = QT2m[0:32, :, :].rearrange("p (hp h2 t) f -> p hp h2 t f", hp=4, h2=2)[:, :, h2, :, h2 * 64:(h2 + 1) * 64]
        src = Qb[0:32, :, :].rearrange("p t (hp f) -> p hp t f", hp=4)[:, :, :, h2 * 64:(h2 + 1) * 64]
        P.op("dve", lambda h, dst=dst, src=src: h.tensor_copy(dst, src), r=[Qb], w=[QT2m])
    A.release()
    A.mark()
    pti = A.alloc([128], I32, "pti")
    ptf = A.alloc([128], F32, "ptf")
    pid = A.alloc([NPG], F32, "pid")
    oh = A.alloc([NPG], F32, "oh")
    P.dma("sp", pti[0:32, :], Dm["ptab"], w=[pti])
    P.dma("sp", pid[0:32, :], Dm["pids"], w=[pid])
    P.op("dve", lambda h: h.tensor_copy(ptf[0:32, :], pti[0:32, :]), r=[pti], w=[ptf])
    P.op("dve", lambda h: h.memset(oh[0:32, :], 0.0), w=[oh])
    A.mark()
    E = A.alloc([16, NPG], BF16, "E")
    tmo = A.alloc([NPG], F32, "tmo")
    for l0 in range(0, 128, 16):
        in0 = pid[0:32, :].unsqueeze(1).broadcast_to([32, 16, NPG])
        in1 = ptf[0:32, l0:l0 + 16].unsqueeze(2).broadcast_to([32, 16, NPG])
        P.op("dve", lambda h, in0=in0, in1=in1: h.tensor_tensor(E[0:32, :, :], in0, in1, ALU.is_equal), r=[pid, ptf], w=[E])
        P.op("dve", lambda h: h.tensor_reduce(tmo[0:32, :], E[0:32, :, :].rearrange("p l c -> p c l"), AX.X, ALU.add), r=[E], w=[tmo])
        P.op("dve", lambda h: h.tensor_tensor(oh[0:32, :], oh[0:32, :], tmo[0:32, :], ALU.add), r=[oh, tmo], w=[oh])
    A.release()
    P.op("dve", lambda h: h.tensor_copy(OHT[0:32, :], oh[0:32, :]), r=[oh], w=[OHT])
    A.release()
    if cfg.get("sa_stop", 9) <= 3:
        P.op("dve", lambda h: h.memset(osT[:, :, :], 0.0), w=[osT])
        Dm["d_extra"] = [d_q] + ([d_tab] if 3 >= 4 else []) + ([d_full] if 3 >= 5 else [])
        return
    A.mark()
    Qsel = A.alloc([320, 64], BF16, "Qsel")
    biasfull = A.alloc([512], F32, "biasfull")
    zb8 = A.alloc([512], F32, "zb8")
    P.op("dve", lambda h: h.tensor_copy(biasfull[:, :].rearrange("p (g hh t) -> p g hh t", g=8, hh=8), bias_bc[:, :].unsqueeze(1).unsqueeze(3).broadcast_to([128, 8, 8, 8])), r=[bias_bc], w=[biasfull])
    kst = [A.alloc([512], F32, "kst%d" % i) for i in range(2)]
    vst = [A.alloc([512], F32, "vst%d" % i) for i in range(2)]
    KT = [A.alloc([4, 128], BF16, "KT%d" % i) for i in range(2)]
    vb8 = [A.alloc([8, 512], BF16, "vb8_%d" % i) for i in range(2)]
    e8 = A.alloc([512], F32, "e8")
    sp8 = [A.alloc([512], BF16, "sp8_%d" % i) for i in range(2)]
    A8 = [A.alloc([512], BF16, "A8_%d" % i) for i in range(2)]
    ot = [A.alloc([512], F32, "ot%d" % i) for i in range(1)] * 2
    lall = [A.alloc([512], F32, "lall%d" % i) for i in range(1)] * 2
    d_tab = T(None, "d_tab")
    tab = Dm["tab_loc"]
    NGRP = cfg.get("ngrp", NPG // 8)
    for grp in range(NGRP):
        if grp % 40 == 0:
            half = grp // 40
            for c in range(64):
                pq = ps[c % 4]
                P.op("pe", lambda h, pq=pq, c=c, half=half: h.matmul(pq[:, 0:320], lhsT=QT2m[0:32, c, :], rhs=OHT[0:32, half * 320:(half + 1) * 320], start=True, stop=True), r=[QT2m, OHT], w=[pq], skip_self=True)
                cp(ACTDVE[c % 2], Qsel[:, :, c], pq[:, 0:320], [pq], [Qsel])
        pz, po, pl = ps[grp % 2], ps[2 + grp % 2], ps[6 + grp % 2]
        vbt, spt, At, ott, lat = vb8[grp % 2], sp8[grp % 2], A8[grp % 2], ot[grp % 2], lall[grp % 2]
        for i8 in range(8):
            pg = grp * 8 + i8
            kt_, vt_, KTt = kst[pg % 2], vst[pg % 2], KT[pg % 2]
            P.dma("sp", kt_[:, :], Dm["ck"][pg], w=[kt_])
            P.dma("sp", vt_[:, :], Dm["cv"][pg], w=[vt_])
            pk = ps[4 + pg % 2]
            for hp in range(4):
                P.op("pe", lambda h, pk=pk, hp=hp, kt_=kt_: h.transpose(pk[:, hp * 128:(hp + 1) * 128], kt_[:, hp * 128:(hp + 1) * 128], ident_f[:, :]), r=[kt_, ident_f], w=[pk], skip_self=True)
            cp(ACTDVE[pg % 2], KTt[:, :, :], pk[:, :].rearrange("p (a b) -> p a b", a=4), [pk], [KTt])
            P.op("pool", lambda h, vbt=vbt, i8=i8, vt_=vt_: h.tensor_copy(vbt[:, i8, :], vt_[:, :]), r=[vt_], w=[vbt])
            for hp in range(4):
                P.op("pe", lambda h, pz=pz, i8=i8, hp=hp, KTt=KTt, pg=pg: h.matmul(pz[:, i8 * 64 + hp * 16: i8 * 64 + (hp + 1) * 16], lhsT=KTt[:, hp, :], rhs=Qsel[:, pg % 320, hp * 16:(hp + 1) * 16],
                                                                                start=(i8 == 0 and hp == 0), stop=False, skip_group_check=True), r=[KTt, Qsel], w=[pz], skip_self=True)
        P.op("dve", lambda h, pz=pz: h.tensor_tensor(zb8[:, :], pz[:, :], biasfull[:, :], ALU.add), r=[pz, biasfull], w=[zb8])
        P.op("act", lambda h: h.activation(e8[:, :], zb8[:, :], AF.Exp), r=[zb8], w=[e8])
        P.op("act", lambda h, spt=spt: h.activation(spt[:, :], e8[:, :], AF.Ln, bias=1.0), r=[e8], w=[spt])
        P.op("pe", lambda h, pz=pz, spt=spt: h.matmul(pz[:, :], lhsT=negU[:, :], rhs=spt[:, :], start=True, stop=True), r=[negU, spt], w=[pz], skip_self=True)
        P.op("dve", lambda h, pz=pz: h.tensor_tensor(zb8[:, :], pz[:, :], zb8[:, :], ALU.add), r=[pz, zb8], w=[zb8])
        P.op("pe", lambda h, pl=pl, spt=spt: h.matmul(pl[:, :], lhsT=ones_b[:, :], rhs=spt[:, :], start=True, stop=True), r=[ones_b, spt], w=[pl], skip_self=True)
        P.op("dve", lambda h, lat=lat, pl=pl: h.tensor_copy(lat[:, :], pl[:, :]), r=[pl], w=[lat])
        P.op("act", lambda h, At=At: h.activation(At[:, :], zb8[:, :], AF.Exp), r=[zb8], w=[At])
        for i8 in range(8):
            for hh in range(8):
                c0 = i8 * 64 + hh * 8
                P.op("pe", lambda h, po=po, c0=c0, vbt=vbt, i8=i8, hh=hh, At=At: h.matmul(po[0:64, c0:c0 + 8], lhsT=vbt[:, i8, hh * 64:(hh + 1) * 64], rhs=At[:, c0:c0 + 8],
                                                                               start=(i8 == 0 and hh == 0), stop=False, skip_group_check=True), r=[vbt, At], w=[po], skip_self=True)
        P.op("dve", lambda h, ott=ott, po=po: h.tensor_copy(ott[0:64, :], po[0:64, :]), r=[po], w=[ott])
        P.dma("pool", tab[grp * 8:(grp + 1) * 8, 0:4096].rearrange("g (d c) -> d g c", d=64), ott[0:64, :].rearrange("p (g c) -> p g c", g=8), r=[ott], w=[d_tab])
        P.dma("pool", tab[grp * 8:(grp + 1) * 8, 4096:TABW].rearrange("(o g) c -> o g c", o=1), lat[0:1, :].rearrange("p (g c) -> p g c", g=8), r=[lat], w=[d_tab])
    A.release()
    if cfg.get("sa_stop", 9) <= 4:
        P.op("dve", lambda h: h.memset(osT[:, :, :], 0.0), w=[osT])
        Dm["d_extra"] = [d_q] + ([d_tab] if 4 >= 4 else []) + ([d_full] if 4 >= 5 else [])
        return
    d_full = T(None, "d_full")
    P.coll("pool", lambda h: h.collective_compute("AllGather", ALU.bypass, replica_groups=[list(range(8))], ins=[Dm["tab_loc"].opt()], outs=[Dm["tab_full"].opt()]), r=[d_tab], w=[d_full])
    if cfg.get("sa_stop", 9) <= 5:
        P.op("dve", lambda h: h.memset(osT[:, :, :], 0.0), w=[osT])
        Dm["d_extra"] = [d_q] + ([d_tab] if 5 >= 4 else []) + ([d_full] if 5 >= 5 else [])
        return
    si = A.alloc([64], I32, "si")
    sf = A.alloc([64], F32, "sf")
    maskN = A.alloc([32], BF16, "maskN")
    tmp32 = A.alloc([32], F32, "tmp32")
    for p0 in (0, 64):
        P.op("pool", lambda h, p0=p0: h.iota(si[p0:p0 + 64, 0:32], pattern=[[1, 32]], base=0, channel_multiplier=-1), w=[si])
        P.op("pool", lambda h, p0=p0: h.iota(si[p0:p0 + 64, 32:33], pattern=[[0, 1]], base=0, channel_multiplier=1), w=[si])
    P.op("dve", lambda h: h.tensor_copy(sf[:, 0:33], si[:, 0:33]), r=[si], w=[sf])
    sm = A.alloc([2], F32, "sm")
    smi = A.alloc([2], I32, "smi")
    P.op("dve", lambda h: h.tensor_single_scalar(smi[:, 0:1], si[:, 32:33], 7, ALU.bitwise_and), r=[si], w=[smi])
    P.op("dve", lambda h: h.tensor_copy(sm[:, 0:1], smi[:, 0:1]), r=[smi], w=[sm])
    P.op("dve", lambda h: h.tensor_scalar(tmp32[:, :], sf[:, 0:32], sm[:, 0:1], None, ALU.add), r=[sf, sm], w=[tmp32])
    mk1 = A.alloc([32], F32, "mk1")
    P.op("dve", lambda h: h.tensor_single_scalar(mk1[:, :], tmp32[:, :], 8.0, ALU.is_lt), r=[tmp32], w=[mk1])
    P.op("dve", lambda h: h.tensor_single_scalar(tmp32[:, :], sf[:, 0:32], 0.0, ALU.is_gt), r=[sf], w=[tmp32])
    P.op("dve", lambda h: h.tensor_tensor(maskN[:, :], mk1[:, :], tmp32[:, :], ALU.mult), r=[mk1, tmp32], w=[maskN])
    mask4 = maskN[:, :].unsqueeze(1).broadcast_to([128, 4, 32])
    biasN = A.alloc([2, 128], F32, "biasN")
    for hh in range(2):
        P.op("dve", lambda h, hh=hh: h.tensor_copy(biasN[:, hh, :].rearrange("p (hp t) -> p hp t", hp=4), bias_bc[:, :].rearrange("p (hp h2) -> p hp h2", h2=2)[:, :, hh].unsqueeze(2).broadcast_to([128, 4, 32])), r=[bias_bc], w=[biasN])
    vt2f = A.alloc([512], F32, "vt2f")
    vtok2 = A.alloc([512], BF16, "vtok2")
    ks64 = A.alloc([4, 64], BF16, "ks64")
    P.op("dve", lambda h: h.memset(ks64[:, :, :], 0.0), w=[ks64])
    P.op("dve", lambda h: h.tensor_copy(ks64[:, :, 0:32], ks[:, :, :]), r=[ks], w=[ks64])
    P.op("dve", lambda h: h.memset(vtok2[:, :], 0.0), w=[vtok2])
    for p0 in (0, 64):
        P.dma("sp", vt2f[p0:p0 + 32, :], Dm["v_s"], r=[Dm["d_vs"]], w=[vt2f])
    for p0 in (0, 64):
        P.op("dve", lambda h, p0=p0: h.tensor_copy(vtok2[p0:p0 + 32, :], vt2f[p0:p0 + 32, :]), r=[vt2f], w=[vtok2])
    zb = A.alloc([128], F32, "zbn")
    en = A.alloc([128], F32, "en")
    spb = A.alloc([128], BF16, "spbn")
    An = A.alloc([128], BF16, "An")
    lnw = A.alloc([2, 128], F32, "lnw")
    pon = ps[2]
    for hh in range(2):
        p0 = 64 * hh
        sl = slice(p0, p0 + 64)
        pzn, pcn = ps[hh], ps[4 + hh]
        for hp in range(4):
            P.op("pe", lambda h, pzn=pzn, sl=sl, hp=hp, p0=p0: h.matmul(pzn[sl, hp * 32:(hp + 1) * 32], lhsT=ks64[p0:p0 + 64, hp, :], rhs=qs[p0:p0 + 64, hp, :], start=(hp == 0), stop=False, skip_group_check=True), r=[ks, qs], w=[pzn], skip_self=True)
        P.op("dve", lambda h, pzn=pzn, sl=sl, hh=hh: h.tensor_tensor(zb[sl, :], pzn[sl, 0:128], biasN[sl, hh, :], ALU.add), r=[pzn, biasN], w=[zb])
        P.op("act", lambda h, sl=sl: h.activation(en[sl, :], zb[sl, :], AF.Exp), r=[zb], w=[en])
        P.op("act", lambda h, sl=sl: h.activation(en[sl, :], en[sl, :], AF.Ln, bias=1.0), r=[en], w=[en])
        P.op("dve", lambda h, sl=sl: h.tensor_tensor(spb[sl, :].rearrange("p (a t) -> p a t", a=4), en[sl, :].rearrange("p (a t) -> p a t", a=4), mask4[sl], ALU.mult), r=[en, maskN], w=[spb])
        P.op("pe", lambda h, pcn=pcn, sl=sl: h.matmul(pcn[sl, 0:128], lhsT=negU[sl, sl], rhs=spb[sl, :], start=True, stop=True), r=[negU, spb], w=[pcn], skip_self=True)
        P.op("dve", lambda h, pcn=pcn, sl=sl: h.tensor_tensor(zb[sl, :], pcn[sl, 0:128], zb[sl, :], ALU.add), r=[pcn, zb], w=[zb])
        P.op("act", lambda h, sl=sl: h.activation(en[sl, :], zb[sl, :], AF.Exp), r=[zb], w=[en])
        P.op("dve", lambda h, sl=sl: h.tensor_tensor(An[sl, :].rearrange("p (a t) -> p a t", a=4), en[sl, :].rearrange("p (a t) -> p a t", a=4), mask4[sl], ALU.mult), r=[en, maskN], w=[An])
        pln = ps[6 + hh]
        P.op("pe", lambda h, pln=pln, sl=sl: h.matmul(pln[:, 0:128], lhsT=ones_b[sl, :], rhs=spb[sl, :], start=True, stop=True), r=[ones_b, spb], w=[pln], skip_self=True)
        P.op("dve", lambda h, pln=pln, hh=hh: h.tensor_copy(lnw[:, hh, :], pln[:, 0:128]), r=[pln], w=[lnw])
        for hp in range(4):
            hd = 2 * hp + hh
            P.op("pe", lambda h, sl=sl, p0=p0, hp=hp, hd=hd: h.matmul(pon[p0:p0 + 64, hp * 32:(hp + 1) * 32], lhsT=vtok2[sl, hd * 64:(hd + 1) * 64], rhs=An[sl, hp * 32:(hp + 1) * 32], start=(hp == 0), stop=False, skip_group_check=True),
                 r=[vtok2, An], w=[pon], skip_self=True)
    if cfg.get("sa_stop", 9) <= 6:
        P.op("dve", lambda h: h.memset(osT[:, :, :], 0.0), w=[osT])
        Dm["d_extra"] = [d_q] + ([d_tab] if 6 >= 4 else []) + ([d_full] if 6 >= 5 else [])
        return
    opast = A.alloc([4, 32], F32, "opast")
    idx = A.alloc([1], I32, "idx")
    G = A.alloc([TABW], F32, "G")
    wv = A.alloc([64], F32, "wv")
    Lb = A.alloc([64], BF16, "Lb")
    sufUb = A.alloc([128], BF16, "sufUb")
    P.op("dve", lambda h: h.tensor_copy(sufUb[:, :], sufU[:, :]), r=[sufU], w=[sufUb])
    X = A.alloc([4096], BF16, "X")
    orow = A.alloc([512], F32, "orow")
    d_osc = T(None, "d_osc")
    for bl in range(4):
        P.dma("sp", idx[:, :], Dm["pt_own"][bl].rearrange("(p o) -> p o", o=1), w=[idx])
        P.custom_dma("pool", lambda h: h.indirect_dma_start(out=G[:, :], out_offset=None, in_=Dm["tab_full"], in_offset=bass.IndirectOffsetOnAxis(ap=idx[:, :], axis=0)), r=[idx, d_full], w=[G])
        pa = ps[6]
        P.op("dve", lambda h: h.tensor_copy(Lb[:, :], G[:, 4096:TABW]), r=[G], w=[Lb])
        P.op("pe", lambda h: h.matmul(pa[:, 0:64], lhsT=sufUb[:, :], rhs=Lb[:, :], start=True, stop=True), r=[sufUb, Lb], w=[pa], skip_self=True)
        lv_ = lnw[:, :, :].rearrange("p h2 (hp tg) -> p hp h2 tg", hp=4)[:, :, :, bl * 8:(bl + 1) * 8]
        P.op("dve", lambda h, lv_=lv_: h.tensor_tensor(wv[:, :].rearrange("p (hp h2 t) -> p hp h2 t", hp=4, h2=2), pa[:, 0:64].rearrange("p (hp h2 t) -> p hp h2 t", hp=4, h2=2), lv_, ALU.add), r=[pa, lnw], w=[wv])
        P.op("act", lambda h: h.activation(wv[:, :], wv[:, :], AF.Exp, scale=-1.0), r=[wv], w=[wv])
        P.op("dve", lambda h: h.tensor_tensor(X[:, :].rearrange("p (d c) -> p d c", d=64), G[:, 0:4096].rearrange("p (d c) -> p d c", d=64), wv[:, :].unsqueeze(1).broadcast_to([128, 64, 64]), ALU.mult), r=[G, wv], w=[X])
        for j in range(8):
            pr = ps[4 + j % 2]
            P.op("pe", lambda h, pr=pr, j=j: h.matmul(pr[:, :], lhsT=ones_b[:, :], rhs=X[:, j * 512:(j + 1) * 512], start=True, stop=True), r=[ones_b, X], w=[pr], skip_self=True)
            cp(ACTDVE[j % 2], orow[:, :], pr[:, :], [pr], [orow])
            P.dma("pool", Dm["osc"][bl:bl + 1, j * 512:(j + 1) * 512], orow[0:1, :], r=[orow], w=[d_osc])
        for h2 in range(2):
            src = Dm["osc"][bl].rearrange("(d hp h2 t) -> h2 d hp t", d=64, hp=4, h2=2)[h2]
            P.dma("sp", opast[h2 * 64:(h2 + 1) * 64, :, bl * 8:(bl + 1) * 8], src, r=[d_osc], w=[opast], allow_slow_non_contiguous=True)
    P.op("dve", lambda h: h.tensor_tensor(osT[:, :, :].rearrange("p a t -> p (a t)"), pon[:, 0:128], opast[:, :, :].rearrange("p a t -> p (a t)"), ALU.add), r=[pon, opast], w=[osT])
    Dm["d_extra"] = [d_tab, d_full, d_osc, d_q]
    A.release()


from concourse.bass_utils import run_bass_kernel_spmd

_NC_CACHE = {}


def kernel(**inputs):
    n = 8
    cfg = dict(nseq=2, sample=True, s_attn=True)
    if "nc" not in _NC_CACHE:
        _NC_CACHE["nc"] = build(cfg)
    nc = _NC_CACHE["nc"]
    f = lambda a: np.ascontiguousarray(np.asarray(a, dtype=np.float32))
    shared = {}
    for k in ["w_in_ab", "q_norm", "k_norm", "sb_bias", "mu_rw", "w0", "a0", "k_k", "k_a", "lnx_w", "lnx_b", "w2", "a2", "g2",
              "w_out_ab", "w_in_c", "conv_w", "w_out_c"]:
        shared[k] = f(inputs[k][0])
    shared["r_k"] = f(np.asarray(inputs["r_k"][0]).reshape(512))
    for k in ["norm_mix", "norm_ffn", "w_gate", "w_up", "w_down"]:
        shared[k] = f(inputs[k])
    xp = np.asarray(inputs["x_prompt"], dtype=np.float32)
    xs = np.asarray(inputs["x_sample"], dtype=np.float32)
    swkv = np.asarray(inputs["state_wkv"], dtype=np.float32)[0]
    sshift = np.asarray(inputs["state_shift"], dtype=np.float32)[0]
    sconv = np.asarray(inputs["state_conv"], dtype=np.float32)[0]
    ptab = np.ascontiguousarray(np.asarray(inputs["page_table"]).astype(np.int32))
    ck = np.asarray(inputs["cache_k"], dtype=np.float32)[0].reshape(8 * NPG, 128, 512)
    cv = np.asarray(inputs["cache_v"], dtype=np.float32)[0].reshape(8 * NPG, 128, 512)
    shared["xs_all"] = np.ascontiguousarray(xs.reshape(256, D))
    shared["ptab"] = ptab
    in_maps = []
    for c in range(n):
        m = dict(shared)
        m["xp"] = np.ascontiguousarray(xp[2 * c:2 * c + 2])
        m["xs_own"] = np.ascontiguousarray(xs[4 * c:4 * c + 4].reshape(32, D))
        m["state_wkv"] = np.ascontiguousarray(swkv[4 * c:4 * c + 4])
        m["state_shift"] = np.ascontiguousarray(sshift[4 * c:4 * c + 4])
        m["state_conv"] = np.ascontiguousarray(sconv[4 * c:4 * c + 4])
        m["pt_own"] = np.ascontiguousarray(ptab[4 * c:4 * c + 4])
        m["pids"] = np.ascontiguousarray(np.broadcast_to((NPG * c + np.arange(NPG, dtype=np.float32))[None, :], (32, NPG)))
        m["ck"] = np.ascontiguousarray(ck[NPG * c:NPG * (c + 1)])
        m["cv"] = np.ascontiguousarray(cv[NPG * c:NPG * (c + 1)])
        in_maps.append(m)
    res = run_bass_kernel_spmd(nc, in_maps, core_ids=list(range(n)))
    R_ = res.results
    cat = lambda name: np.concatenate([np.asarray(r[name], dtype=np.float32) for r in R_], axis=0)
    y_prompt = cat("y_p")
    k_prompt = cat("k_p").reshape(1, 16, S, 8, 64)
    v_prompt = cat("v_p").reshape(1, 16, S, 8, 64)
    wkv_prompt = cat("wkv_p").reshape(1, 16, 8, 64, 64)
    shift_prompt = cat("shift_p").reshape(1, 16, PRW)
    conv_prompt = cat("conv_p").reshape(1, 16, 2, D)
    y_sample = cat("y_s").reshape(32, 8, D)
    k_sample = cat("k_s").reshape(1, 32, 8, 8, 64)
    v_sample = cat("v_s").reshape(1, 32, 8, 8, 64)
    wkv_sample = cat("wkv_s").reshape(1, 32, 8, 64, 64)
    shift_sample = cat("shift_s").reshape(1, 32, PRW)
    conv_sample = cat("conv_s").reshape(1, 32, 2, D)
    return (y_prompt, y_sample, k_prompt, v_prompt, k_sample, v_sample, wkv_prompt, wkv_sample,
            shift_prompt, shift_sample, conv_prompt, conv_sample)
```
